# Optimizing a Trainium2 kernel written in Bass

```python
import math
import jax
import jax.numpy as jnp
from jax import lax
import numpy as np

D_MODEL = 2048
BATCH = 8
SEQ = 2048
DEPTH = 2

N_META = 16
BLK = 128
PAD = BLK - N_META
CHUNK = 64
ROPE_THETA = 500000.0
ROPE_FRAC = 4
NORM_EPS = 1e-6
NEG = -1e30

N_BRANCH = 4
BRANCH_W = D_MODEL // 2
DA_HEADS = 8
DA_DV = BRANCH_W // DA_HEADS
DA_DK = DA_DV // 2
HG_HEADS = 8
HG_DK = 128
HG_DV = BRANCH_W // HG_HEADS
DS_HEADS = 8
DS_KV_HEADS = 2
DS_DH = BRANCH_W // DS_HEADS
IDX_HEADS = 16
IDX_DH = 64
TOPK_MAX = 256
ML_HEADS = 4
ML_DH = BRANCH_W // ML_HEADS
CONV_W = 4
D_FF = 256 * ((8 * D_MODEL // 3 + 255) // 256)

IN_GROUPS = (
    ("da_q", DA_HEADS * 2 * DA_DK), ("da_k", DA_HEADS * 2 * DA_DK), ("da_v", DA_HEADS * DA_DV),
    ("hg_q", HG_HEADS * HG_DK), ("hg_f", HG_HEADS * HG_DK), ("hg_i", HG_HEADS * HG_DV), ("hg_g", HG_HEADS * HG_DV),
    ("ds_q", DS_HEADS * DS_DH), ("ds_k", DS_KV_HEADS * DS_DH), ("ds_v", DS_KV_HEADS * DS_DH),
    ("ix_q", IDX_HEADS * IDX_DH), ("ix_k", IDX_DH), ("ix_w", IDX_HEADS),
    ("ml_q", ML_HEADS * ML_DH), ("ml_k", ML_HEADS * ML_DH), ("ml_v", ML_HEADS * ML_DH), ("ml_o", ML_HEADS * ML_DH),
    ("ml_i", ML_HEADS), ("ml_f", ML_HEADS),
    ("gate", N_BRANCH * D_MODEL),
)
_IN_STARTS = np.concatenate([[0], np.cumsum([w for _, w in IN_GROUPS])]).astype(int).tolist()
IN_SLICES = {name: (_IN_STARTS[i], _IN_STARTS[i + 1]) for i, (name, _) in enumerate(IN_GROUPS)}
IN_COLS = _IN_STARTS[-1]

kernel_name = "hybrid_gated_diffattn_hgrn2_dsa_mlstm"

F32 = jnp.float32


def rmsnorm(x, g):
    xf = x.astype(F32)
    y = xf * lax.rsqrt(jnp.mean(xf * xf, axis=-1, keepdims=True) + NORM_EPS)
    return (y * g.astype(F32)).astype(x.dtype)


def proj(h, w_in, name):
    a, b = IN_SLICES[name]
    return h @ w_in[:, a:b]


def rope_partial(x, pos):
    d = x.shape[-1]
    r = d // ROPE_FRAC
    half = r // 2
    inv = ROPE_THETA ** (-jnp.arange(half, dtype=F32) / half)
    ang = pos.astype(F32)[:, None] * inv[None, :]
    cos = jnp.cos(ang)[:, None, :]
    sin = jnp.sin(ang)[:, None, :]
    xf = x.astype(F32)
    x1, x2 = xf[..., :half], xf[..., half:r]
    out = jnp.concatenate([x1 * cos - x2 * sin, x2 * cos + x1 * sin, xf[..., r:]], axis=-1)
    return out.astype(x.dtype)


def swiglu(h, w13, w2):
    g = h @ w13[:, :D_FF]
    u = h @ w13[:, D_FF:]
    return (jax.nn.silu(g) * u) @ w2


def causal_conv(x, w):
    return lax.conv_general_dilated(
        x, w[:, None, :].astype(x.dtype), window_strides=(1,), padding=[(CONV_W - 1, 0)],
        dimension_numbers=("NWC", "WIO", "NWC"), feature_group_count=x.shape[-1])


def diff_attention(h, w_in, lam, sub_g, layer, pos, valid):
    B, P, _ = h.shape
    q = rope_partial(proj(h, w_in, "da_q").reshape(B, P, DA_HEADS * 2, DA_DK), pos)
    k = rope_partial(proj(h, w_in, "da_k").reshape(B, P, DA_HEADS * 2, DA_DK), pos)
    q = q.reshape(B, P, DA_HEADS, 2, DA_DK)
    k = k.reshape(B, P, DA_HEADS, 2, DA_DK)
    v = proj(h, w_in, "da_v").reshape(B, P, DA_HEADS, DA_DV)
    lam_init = 0.8 - 0.6 * math.exp(-0.3 * layer)
    lf = lam.astype(F32)
    lam_full = jnp.exp(jnp.sum(lf[0] * lf[1])) - jnp.exp(jnp.sum(lf[2] * lf[3])) + lam_init
    scale = DA_DK ** -0.5
    kidx = jnp.arange(P)
    nb = P // BLK
    qb = q.reshape(B, nb, BLK, DA_HEADS, 2, DA_DK).transpose(1, 0, 2, 3, 4, 5)
    starts = jnp.arange(nb) * BLK

    def block(args):
        qblk, s0 = args
        qi = s0 + jnp.arange(BLK)
        s = jnp.einsum("bqhcd,bkhcd->bhcqk", qblk, k, preferred_element_type=F32) * scale
        allowed = (kidx[None, :] <= qi[:, None]) & (valid[None, :] | (kidx[None, :] == qi[:, None]))
        p = jax.nn.softmax(jnp.where(allowed, s, NEG), axis=-1)
        pd = p[:, :, 0] - lam_full * p[:, :, 1]
        return jnp.einsum("bhqk,bkhe->bqhe", pd.astype(v.dtype), v)

    o = lax.map(block, (qb, starts))
    o = o.transpose(1, 0, 2, 3, 4).reshape(B, P, DA_HEADS, DA_DV)
    o = rmsnorm(o, sub_g) * (1.0 - lam_init)
    return o.reshape(B, P, BRANCH_W).astype(h.dtype)


def hgrn2(h, w_in, lb, norm_g, valid):
    B, P, _ = h.shape
    q = proj(h, w_in, "hg_q").reshape(B, P, HG_HEADS, HG_DK).astype(F32)
    fp = proj(h, w_in, "hg_f").reshape(B, P, HG_HEADS, HG_DK).astype(F32)
    v = proj(h, w_in, "hg_i").reshape(B, P, HG_HEADS, HG_DV).astype(F32)
    g = proj(h, w_in, "hg_g").reshape(B, P, HG_HEADS, HG_DV).astype(F32)
    lb = lb.reshape(HG_HEADS, HG_DK)
    logf = jnp.logaddexp(jnp.log(lb), jnp.log1p(-lb) + jax.nn.log_sigmoid(fp))
    logf = jnp.where(valid[None, :, None, None], logf, 0.0)
    kk = -jnp.expm1(logf)
    nc = P // CHUNK

    def to_chunks(t):
        return t.reshape(B, nc, CHUNK, HG_HEADS, t.shape[-1]).transpose(1, 0, 3, 2, 4)

    tri = jnp.tril(jnp.ones((CHUNK, CHUNK), bool))[:, :, None]

    def step(S, xs):
        qc, kc, vc, gc = xs
        b = jnp.cumsum(gc, axis=2)
        inter = jnp.einsum("bhtk,bhkv->bhtv", qc * jnp.exp(b), S)
        diff = b[:, :, :, None, :] - b[:, :, None, :, :]
        dec = jnp.where(tri, jnp.exp(jnp.where(tri, diff, 0.0)), 0.0)
        a = jnp.einsum("bhtk,bhsk,bhtsk->bhts", qc, kc, dec)
        o = inter + jnp.einsum("bhts,bhsv->bhtv", a, vc)
        bl = b[:, :, -1:, :]
        S = jnp.exp(bl[:, :, 0])[..., None] * S + jnp.einsum("bhsk,bhsv->bhkv", kc * jnp.exp(bl - b), vc)
        return S, o

    S0 = jnp.zeros((B, HG_HEADS, HG_DK, HG_DV), F32)
    _, o = lax.scan(step, S0, (to_chunks(q), to_chunks(kk), to_chunks(v), to_chunks(logf)))
    o = o.transpose(1, 0, 3, 2, 4).reshape(B, P, HG_HEADS, HG_DV)
    o = rmsnorm(o, norm_g) * jax.nn.silu(g)
    return o.reshape(B, P, BRANCH_W).astype(h.dtype)


def dsa_attention(h, w_in, pos):
    B, P, _ = h.shape
    q = rope_partial(proj(h, w_in, "ds_q").reshape(B, P, DS_HEADS, DS_DH), pos)
    k = rope_partial(proj(h, w_in, "ds_k").reshape(B, P, DS_KV_HEADS, DS_DH), pos)
    v = proj(h, w_in, "ds_v").reshape(B, P, DS_KV_HEADS, DS_DH)
    iq = rope_partial(proj(h, w_in, "ix_q").reshape(B, P, IDX_HEADS, IDX_DH), pos).astype(F32)
    ik = rope_partial(proj(h, w_in, "ix_k").reshape(B, P, 1, IDX_DH), pos)[:, :, 0].astype(F32)
    iw = proj(h, w_in, "ix_w").astype(F32) * (IDX_HEADS ** -0.5 * IDX_DH ** -0.5)
    n_real = P - BLK
    topk = min(TOPK_MAX, n_real // 4)
    k_meta, v_meta = k[:, PAD:BLK], v[:, PAD:BLK]
    k_real, v_real, ik_real = k[:, BLK:], v[:, BLK:], ik[:, BLK:]
    meta_idx = jnp.arange(PAD, BLK)
    real_idx = jnp.arange(n_real) + BLK
    G = DS_HEADS // DS_KV_HEADS
    nb = P // BLK
    qb = q.reshape(B, nb, BLK, DS_KV_HEADS, G, DS_DH).transpose(1, 0, 2, 3, 4, 5)
    iqb = iq.reshape(B, nb, BLK, IDX_HEADS, IDX_DH).transpose(1, 0, 2, 3, 4)
    iwb = iw.reshape(B, nb, BLK, IDX_HEADS).transpose(1, 0, 2, 3)
    starts = jnp.arange(nb) * BLK
    scale = DS_DH ** -0.5
    gather = jax.vmap(lambda t, i: t[i])

    def block(args):
        qblk, iqblk, iwblk, s0 = args
        qi = s0 + jnp.arange(BLK)
        isc = jax.nn.relu(jnp.einsum("bqhd,bsd->bqhs", iqblk, ik_real))
        isc = jnp.einsum("bqhs,bqh->bqs", isc, iwblk)
        isc = jnp.where(real_idx[None, :] <= qi[:, None], isc, NEG)
        _, sel = lax.top_k(isc, topk)
        k_sel = gather(k_real, sel)
        v_sel = gather(v_real, sel)
        sel_ok = (sel + BLK) <= qi[None, :, None]
        meta_ok = meta_idx[None, :] <= qi[:, None]
        s_meta = jnp.einsum("bqkgd,bmkd->bkgqm", qblk, k_meta, preferred_element_type=F32) * scale
        s_sel = jnp.einsum("bqkgd,bqnkd->bkgqn", qblk, k_sel, preferred_element_type=F32) * scale
        s_meta = jnp.where(meta_ok, s_meta, NEG)
        s_sel = jnp.where(sel_ok[:, None, None], s_sel, NEG)
        p = jax.nn.softmax(jnp.concatenate([s_meta, s_sel], axis=-1), axis=-1).astype(v.dtype)
        return (jnp.einsum("bkgqm,bmkd->bqkgd", p[..., :N_META], v_meta)
                + jnp.einsum("bkgqn,bqnkd->bqkgd", p[..., N_META:], v_sel))

    o = lax.map(block, (qb, iqb, iwb, starts))
    return o.transpose(1, 0, 2, 3, 4, 5).reshape(B, P, BRANCH_W).astype(h.dtype)


def mlstm(h, w_in, conv_w, i_bias, f_bias, valid):
    B, P, _ = h.shape
    qk = jnp.concatenate([proj(h, w_in, "ml_q"), proj(h, w_in, "ml_k")], axis=-1)
    qk = jax.nn.silu(causal_conv(qk, conv_w))
    q = qk[..., :BRANCH_W].reshape(B, P, ML_HEADS, ML_DH).astype(F32)
    k = qk[..., BRANCH_W:].reshape(B, P, ML_HEADS, ML_DH).astype(F32) * (ML_DH ** -0.5)
    v = proj(h, w_in, "ml_v").reshape(B, P, ML_HEADS, ML_DH).astype(F32)
    og = jax.nn.sigmoid(proj(h, w_in, "ml_o").astype(F32)).reshape(B, P, ML_HEADS, ML_DH)
    ig = proj(h, w_in, "ml_i").astype(F32) + i_bias.astype(F32)
    lf = jax.nn.log_sigmoid(proj(h, w_in, "ml_f").astype(F32) + f_bias.astype(F32))
    vm = valid[None, :, None]
    ig = jnp.where(vm, ig, NEG)
    lf = jnp.where(vm, lf, 0.0)
    nc = P // CHUNK

    def vec_chunks(t):
        return t.reshape(B, nc, CHUNK, ML_HEADS, ML_DH).transpose(1, 0, 3, 2, 4)

    def gate_chunks(t):
        return t.reshape(B, nc, CHUNK, ML_HEADS).transpose(1, 0, 3, 2)

    tri = jnp.tril(jnp.ones((CHUNK, CHUNK), bool))

    def step(carry, xs):
        Cm, n, m = carry
        qc, kc, vc, igc, lfc = xs
        b = jnp.cumsum(lfc, axis=-1)
        dlog = jnp.where(tri, b[..., :, None] - b[..., None, :] + igc[..., None, :], NEG)
        inter_log = b + m[..., None]
        m_t = jnp.maximum(jnp.max(dlog, axis=-1), inter_log)
        dw = jnp.exp(dlog - m_t[..., None])
        iw = jnp.exp(inter_log - m_t)
        qkw = jnp.einsum("bhtd,bhsd->bhts", qc, kc) * dw
        num = iw[..., None] * jnp.einsum("bhtk,bhkv->bhtv", qc, Cm) + jnp.einsum("bhts,bhsv->bhtv", qkw, vc)
        den = iw * jnp.einsum("bhtk,bhk->bht", qc, n) + jnp.sum(qkw, axis=-1)
        ht = num / jnp.maximum(jnp.abs(den), jnp.exp(-m_t))[..., None]
        bl = b[..., -1]
        m_new = jnp.maximum(bl + m, jnp.max(bl[..., None] - b + igc, axis=-1))
        ws = jnp.exp(bl[..., None] - b + igc - m_new[..., None])
        decay = jnp.exp(bl + m - m_new)
        Cm = decay[..., None, None] * Cm + jnp.einsum("bhs,bhsk,bhsv->bhkv", ws, kc, vc)
        n = decay[..., None] * n + jnp.einsum("bhs,bhsk->bhk", ws, kc)
        return (Cm, n, m_new), ht

    init = (jnp.zeros((B, ML_HEADS, ML_DH, ML_DH), F32), jnp.zeros((B, ML_HEADS, ML_DH), F32),
            jnp.zeros((B, ML_HEADS), F32))
    _, hs = lax.scan(step, init, (vec_chunks(q), vec_chunks(k), vec_chunks(v), gate_chunks(ig), gate_chunks(lf)))
    hs = hs.transpose(1, 0, 3, 2, 4).reshape(B, P, ML_HEADS, ML_DH)
    return (og * hs).reshape(B, P, BRANCH_W).astype(h.dtype)


def hybrid_mixer(h, w_in, lam, sub_g, lb, hg_g, conv_w, i_bias, f_bias, w_branch, w_out, layer, pos, valid):
    ys = (
        diff_attention(h, w_in, lam, sub_g, layer, pos, valid),
        hgrn2(h, w_in, lb, hg_g, valid),
        dsa_attention(h, w_in, pos),
        mlstm(h, w_in, conv_w, i_bias, f_bias, valid),
    )
    g0, _ = IN_SLICES["gate"]
    merged = None
    for bi, y in enumerate(ys):
        gate = jax.nn.sigmoid((h @ w_in[:, g0 + bi * D_MODEL:g0 + (bi + 1) * D_MODEL]).astype(F32))
        term = gate * (y @ w_branch[bi]).astype(F32)
        merged = term if merged is None else merged + term
    return merged.astype(h.dtype) @ w_out


def setup_inputs(seed: int = 0) -> dict:
    key = jax.random.key(seed)
    ks = jax.random.split(key, 24)

    def nrm(k, shape, scale):
        return jax.random.normal(k, shape, F32) * scale

    def gain(k, shape):
        return 1.0 + 0.02 * jax.random.normal(k, shape, F32)

    return {
        "x": nrm(ks[0], (BATCH, SEQ, D_MODEL), 1.0),
        "meta_tokens": nrm(ks[1], (N_META, D_MODEL), 1.0),
        "ffn1_norm": gain(ks[2], (DEPTH, D_MODEL)),
        "ffn1_w13": nrm(ks[3], (DEPTH, D_MODEL, 2 * D_FF), D_MODEL ** -0.5),
        "ffn1_w2": nrm(ks[4], (DEPTH, D_FF, D_MODEL), D_FF ** -0.5),
        "mix_norm": gain(ks[5], (DEPTH, D_MODEL)),
        "w_in": nrm(ks[6], (DEPTH, D_MODEL, IN_COLS), D_MODEL ** -0.5),
        "da_lambda": nrm(ks[7], (DEPTH, 4, DA_DK), 0.1),
        "da_sub_norm": gain(ks[8], (DEPTH, DA_DV)),
        "hg_lb_logits": nrm(ks[9], (DEPTH, HG_HEADS * HG_DK), 0.5),
        "hg_norm": gain(ks[10], (DEPTH, HG_DV)),
        "ml_conv": nrm(ks[11], (DEPTH, CONV_W, 2 * BRANCH_W), CONV_W ** -0.5),
        "ml_i_bias": nrm(ks[12], (DEPTH, ML_HEADS), 0.1),
        "ml_f_bias": jnp.linspace(3.0, 6.0, ML_HEADS, dtype=F32)[None, :] + nrm(ks[13], (DEPTH, ML_HEADS), 0.1),
        "w_branch": nrm(ks[14], (DEPTH, N_BRANCH, BRANCH_W, D_MODEL), BRANCH_W ** -0.5),
        "w_out": nrm(ks[15], (DEPTH, D_MODEL, D_MODEL), D_MODEL ** -0.5),
        "ffn2_norm": gain(ks[16], (DEPTH, D_MODEL)),
        "ffn2_w13": nrm(ks[17], (DEPTH, D_MODEL, 2 * D_FF), D_MODEL ** -0.5),
        "ffn2_w2": nrm(ks[18], (DEPTH, D_FF, D_MODEL), D_FF ** -0.5),
        "final_norm": gain(ks[19], (D_MODEL,)),
    }


def reference(x, meta_tokens, ffn1_norm, ffn1_w13, ffn1_w2, mix_norm, w_in, da_lambda, da_sub_norm,
              hg_lb_logits, hg_norm, ml_conv, ml_i_bias, ml_f_bias, w_branch, w_out,
              ffn2_norm, ffn2_w13, ffn2_w2, final_norm):
    B = x.shape[0]
    P = x.shape[1] + BLK
    rows = jnp.arange(P)
    valid = rows >= PAD
    pos = jnp.maximum(rows - PAD, 0)
    vmask = valid[None, :, None].astype(x.dtype)
    meta = jnp.broadcast_to(meta_tokens.astype(x.dtype)[None], (B, N_META, D_MODEL))
    h = jnp.concatenate([jnp.zeros((B, PAD, D_MODEL), x.dtype), meta, x], axis=1)
    lb_p = jax.nn.softmax(hg_lb_logits.astype(F32), axis=0)
    lb_all = jnp.maximum(jnp.cumsum(lb_p, axis=0) - lb_p[0:1], 0.0)
    for l in range(DEPTH):
        h = h + 0.5 * swiglu(rmsnorm(h, ffn1_norm[l]), ffn1_w13[l], ffn1_w2[l])
        hm = rmsnorm(h, mix_norm[l]) * vmask
        h = h + hybrid_mixer(hm, w_in[l], da_lambda[l], da_sub_norm[l], lb_all[l], hg_norm[l], ml_conv[l],
                             ml_i_bias[l], ml_f_bias[l], w_branch[l], w_out[l], l, pos, valid)
        h = h + 0.5 * swiglu(rmsnorm(h, ffn2_norm[l]), ffn2_w13[l], ffn2_w2[l])
    return rmsnorm(h, final_norm)[:, BLK:]
```

```python
from contextlib import ExitStack
import math
import numpy as np
import concourse.bass as bass
import concourse.mybir as mybir
from concourse.bass_utils import run_bass_kernel_spmd

F32 = mybir.dt.float32
BF16 = mybir.dt.bfloat16
AF = mybir.ActivationFunctionType
ALU = mybir.AluOpType
AX = mybir.AxisListType

D = 2048
T = 2176
NB = 17
DFF = 5632
NFB = 44
EPS = 1e-6
GROUPS = [(0, 512), (512, 512), (1024, 384), (1408, 384), (1792, 384)]
GROUPS3 = [(0, 768), (768, 768), (1536, 640)]


def chunks(n):
    return [(a, min(512, n - a)) for a in range(0, n, 512)]
IN_GROUPS = (
    ("da_q", 1024), ("da_k", 1024), ("da_v", 1024),
    ("hg_q", 1024), ("hg_f", 1024), ("hg_i", 1024), ("hg_g", 1024),
    ("ds_q", 1024), ("ds_k", 256), ("ds_v", 256),
    ("ix_q", 1024), ("ix_k", 64), ("ix_w", 16),
    ("ml_q", 1024), ("ml_k", 1024), ("ml_v", 1024), ("ml_o", 1024),
    ("ml_i", 4), ("ml_f", 4), ("gate", 8192),
)
IN_OFF = {}
_o = 0
for _n, _w in IN_GROUPS:
    IN_OFF[_n] = _o
    _o += _w
IN_COLS = _o

FM_DAQ, FM_DAK = 0, 8
FM_HGQ, FM_HGF, FM_HGG = 16, 24, 32
FM_DSQ, FM_DSK = 40, 48
FM_IXQ, FM_IXK = 50, 58
FM_MLQ, FM_MLK, FM_MLO = 59, 67, 75
NPRF = 83
NFM = NPRF + 64
NTM = 7
PP_L = 66
NPP = 2 * PP_L + 16
PR_L = 264
NPR = 2 * PR_L
C_IDENT, C_TRIF, C_ONESF, C_NEGD, C_TRI, C_MASK0, C_ONES0, C_HGM, C_RA, C_RS = range(10)
NCST = 10
NTAB = 5


class Buf:
    __slots__ = ("name", "t", "w", "r", "dsem", "dcount")

    def __init__(self, name, t=None):
        self.name = name
        self.t = t
        self.w = None
        self.r = {}
        self.dsem = None
        self.dcount = 0

    def __getitem__(self, idx):
        return self.t[idx]


class _Rec:
    def __init__(self):
        self.calls = []

    def __getattr__(self, name):
        def f(*a, **k):
            self.calls.append((name, a, k))
            return self
        return f


class Sched:
    ENG = ("pe", "act", "dve", "pool", "sp")

    def __init__(self, nc):
        self.nc = nc
        self.es = ExitStack()
        self.scopes = [self.es]
        self.stream = {e: [] for e in self.ENG}
        self.count = {e: 0 for e in self.ENG}
        self.waited = {e: {} for e in self.ENG}
        self.sems = {}
        self.semval = {}
        self.nsem = 0
        for e in self.ENG:
            self.sems[e] = self.es.enter_context(nc.semaphore("s_" + e))
            self.semval[e] = 0
        self.uid = 0
        self.rec = None
        self.scope_bufs = [[]]
        self.free_sems = []

    def push_scope(self):
        es = ExitStack()
        self.scopes.append(es)
        self.scope_bufs.append([])

    def pop_scope(self):
        self.barrier()
        self.scopes.pop().close()
        for b in self.scope_bufs.pop():
            if b.dsem is not None:
                self.free_sems.append(b.dsem)
                b.dsem = None

    def sb(self, name, shape, dt):
        self.uid += 1
        t = self.scopes[-1].enter_context(self.nc.sbuf_tensor("%s_%d" % (name, self.uid), list(shape), dt))
        b = Buf(name, t)
        self.scope_bufs[-1].append(b)
        return b

    def ps(self, name, shape, dt=F32):
        self.uid += 1
        t = self.scopes[-1].enter_context(self.nc.psum_tensor("%s_%d" % (name, self.uid), list(shape), dt))
        return Buf(name, t)

    def dram(self, name, shape, dt, kind="Internal"):
        t = self.nc.dram_tensor(name, list(shape), dt, kind=kind)
        return Buf(name, t)

    def _dsem(self, b):
        if b.dsem is None and self.free_sems:
            b.dsem = self.free_sems.pop()
            b.dcount = self.semval[b.dsem]
        if b.dsem is None:
            self.nsem += 1
            key = "d%d" % self.nsem
            self.sems[key] = self.es.enter_context(self.nc.semaphore(key))
            self.semval[key] = 0
            b.dsem = key
        return b.dsem

    def _deps(self, reads, writes):
        deps = {}

        def add(k, v):
            if deps.get(k, 0) < v:
                deps[k] = v
        for b in reads:
            if b.w is not None:
                add(*b.w)
        for b in writes:
            if b.w is not None:
                add(*b.w)
            for k, v in b.r.items():
                add(k, v)
        return deps

    def _emit_waits(self, eng, deps):
        wd = self.waited[eng]
        for k, v in deps.items():
            if wd.get(k, 0) >= v:
                continue
            wd[k] = v
            sem = self.sems[k]
            self.stream[eng].append(lambda e, sem=sem, v=v: e.wait_ge(sem, v))

    def _commit(self, tok, reads, writes):
        k, v = tok
        for b in reads:
            if b.r.get(k, 0) < v:
                b.r[k] = v
        for b in writes:
            b.w = tok
            b.r = {}

    def op(self, eng, fn, reads=(), writes=()):
        rec = _Rec()
        fn(rec)
        if self.rec is not None:
            self.rec.append(("op", eng, rec.calls, tuple(reads), tuple(writes)))
            return None
        return self._op(eng, rec.calls, reads, writes)

    def _op(self, eng, calls, reads, writes):
        deps = self._deps(reads, writes)
        self._emit_waits(eng, deps)
        self.count[eng] += 1
        self.semval[eng] = self.count[eng]
        tok = (eng, self.count[eng])
        sem = self.sems[eng]

        def play(e, calls=calls, sem=sem):
            for (name, a, k) in calls:
                ins = getattr(e, name)(*a, **k)
            ins.then_inc(sem, 1)
        self.stream[eng].append(play)
        self._commit(tok, reads, writes)
        return tok

    def dma(self, q, out, in_, reads=(), writes=(), **kw):
        if self.rec is not None:
            self.rec.append(("dma", q, out, in_, tuple(reads), tuple(writes), kw))
            return None
        nowait = kw.pop("nowait", False)
        deps = self._deps(reads, writes)
        if not nowait:
            self._emit_waits(q, deps)
        wb = writes[0]
        key = self._dsem(wb)
        wb.dcount += 16
        self.semval[key] = wb.dcount
        tok = (key, wb.dcount)
        sem = self.sems[key]
        self.stream[q].append(
            lambda e, out=out, in_=in_, sem=sem, kw=kw: e.dma_start(out=out, in_=in_, **kw).then_inc(sem, 16))
        self._commit(tok, reads, writes)
        return tok

    def record(self, gen):
        assert self.rec is None
        self.rec = []
        for _ in gen:
            pass
        ops, self.rec = self.rec, None
        return ops

    def merge(self, threads):
        pos = [0] * len(threads)
        total = sum(len(t) for t in threads)
        for _ in range(total):
            best, bf = None, None
            for i, t in enumerate(threads):
                if pos[i] < len(t):
                    f = pos[i] / len(t)
                    if bf is None or f < bf:
                        best, bf = i, f
            o = threads[best][pos[best]]
            pos[best] += 1
            if o[0] == "op":
                self._op(o[1], o[2], o[3], o[4])
            else:
                self.dma(o[1], o[2], o[3], reads=o[4], writes=o[5], **o[6])

    def barrier(self):
        allv = {k: v for k, v in self.semval.items() if v > 0}
        for e in self.ENG:
            self._emit_waits(e, allv)

    def emit(self):
        self.barrier()
        nc = self.nc
        with nc.Block() as block:
            @block.tensor
            def _(e):
                for f in self.stream["pe"]:
                    f(e)

            @block.scalar
            def _(e):
                for f in self.stream["act"]:
                    f(e)

            @block.vector
            def _(e):
                for f in self.stream["dve"]:
                    f(e)

            @block.gpsimd
            def _(e):
                for f in self.stream["pool"]:
                    f(e)

            @block.sync
            def _(e):
                for f in self.stream["sp"]:
                    f(e)
        while self.scopes:
            self.scopes.pop().close()


class K:
    def __init__(self, nc, cfg):
        self.nc = nc
        self.cfg = cfg
        self.S = Sched(nc)
        S = self.S
        IN = "ExternalInput"
        self.xT = S.dram("xT", [16, 128, T], F32, IN)
        self.w13r = S.dram("w13r", [4, NFB, 128, 16 * 256], F32, IN)
        self.w2r = S.dram("w2r", [4, 16, 128, NFB * 128], F32, IN)
        self.gains = S.dram("gains", [128, 112], F32, IN)
        self.outT = S.dram("outT", [16, 128, T], F32, "ExternalOutput")
        self.HT = S.dram("HT", [16, 128, T], F32)
        self.wdummy = Buf("wdummy")
        self.gn = S.sb("gn", [128, 112], F32)
        self.ones = S.sb("ones", [128, 128], BF16)
        S.dma("sp", self.gn[:], self.gains.t.ap(), reads=[self.gains], writes=[self.gn])
        S.op("dve", lambda e: e.memset(self.ones[:], 1.0), writes=[self.ones])

    def load_h(self, src, h, c0, n):
        self.S.dma("sp", h[:, :, 0:n], src.t.ap()[:, :, c0:c0 + n].rearrange("k p t -> p k t"),
                   reads=[src], writes=[h])

    def norm_group(self, h, hn, sq, ps, rs, n, gcol, pad0=False, ho=0):
        S = self.S
        S.op("act", lambda e: e.activation(out=sq[:, 0:16, 0:n], in_=h[:, :, 0:n], func=AF.Square),
             reads=[h], writes=[sq])

        def mm(e):
            for (a, w_) in chunks(n):
                for kc in range(16):
                    i = e.matmul(ps[:, a:a + w_], lhsT=self.ones[:], rhs=sq[:, kc, a:a + w_], start=(kc == 0), stop=(kc == 15))
            return i
        S.op("pe", mm, reads=[sq, self.ones], writes=[ps])
        S.op("dve", lambda e: e.tensor_scalar(out=rs[:, 0:n], in0=ps[:, 0:n], scalar1=1.0 / D, scalar2=EPS,
                                              op0=ALU.mult, op1=ALU.add), reads=[ps], writes=[rs])
        S.op("act", lambda e: e.activation(out=rs[:, 0:n], in_=rs[:, 0:n], func=AF.Ln), reads=[rs], writes=[rs])
        S.op("act", lambda e: e.activation(out=rs[:, 0:n], in_=rs[:, 0:n], func=AF.Exp, scale=-0.5), reads=[rs], writes=[rs])
        if pad0:
            S.op("dve", lambda e: e.memset(rs[:, 0:112], 0.0), reads=[rs], writes=[rs])

        def nrm(e):
            for kc in range(16):
                i = e.scalar_tensor_tensor(out=hn[:, kc, ho:ho + n], in0=h[:, kc, 0:n],
                                           scalar=self.gn[:, gcol + kc:gcol + kc + 1], in1=rs[:, 0:n],
                                           op0=ALU.mult, op1=ALU.mult)
            return i
        S.op("dve", nrm, reads=[h, rs, self.gn], writes=[hn])

    def ffn(self, src, dst, widx, gcol, w16=None):
        S = self.S
        S.push_scope()
        NG = 768
        h = S.sb("h", [128, 16, NG], F32)
        hn = S.sb("hn", [128, 16, NG], BF16)
        aT = S.sb("aT", [128, NFB, NG], BF16)
        rs = S.sb("rs", [128, NG], F32)
        w1 = [S.sb("w1_%d" % i, [128, 16, 256], BF16) for i in range(3)]
        w2 = [S.sb("w2_%d" % i, [128, NFB, 128], BF16) for i in range(2)]
        sg = [S.sb("sg%d" % i, [128, NG], F32) for i in range(2)]
        pg = S.ps("pg", [128, 1024])
        pu = S.ps("pu", [128, 1024])
        po = [S.ps("po%d" % i, [128, 1024]) for i in range(2)]
        nslab = 0
        for (c0, n) in GROUPS3:
            self.load_h(src, h, c0, n)
            self.norm_group(h, hn, aT, po[0], rs, n, gcol)
            for fb in range(NFB):
                sl = w1[nslab % 3]
                nslab += 1
                if w16 is None:
                    S.dma("pool", sl[:].rearrange("p k c -> p (k c)"), self.w13r.t.ap()[widx, fb],
                          reads=[self.wdummy], writes=[sl])
                else:
                    S.dma("sp", sl[:].rearrange("p k c -> p (k c)"), self.w13r16_t.ap()[w16, fb],
                          reads=[self.convBuf], writes=[sl])
                s_g = sg[fb % 2]

                def mmg(e):
                    for (a, w_) in chunks(n):
                        for kc in range(16):
                            i = e.matmul(pg[:, a:a + w_], lhsT=sl[:, kc, 0:128], rhs=hn[:, kc, a:a + w_], start=(kc == 0), stop=(kc == 15))
                    return i

                def mmu(e):
                    for (a, w_) in chunks(n):
                        for kc in range(16):
                            i = e.matmul(pu[:, a:a + w_], lhsT=sl[:, kc, 128:256], rhs=hn[:, kc, a:a + w_], start=(kc == 0), stop=(kc == 15))
                    return i
                S.op("pe", mmg, reads=[sl, hn], writes=[pg])
                S.op("pe", mmu, reads=[sl, hn], writes=[pu])
                S.op("act", lambda e: e.activation(out=s_g[:, 0:n], in_=pg[:, 0:n], func=AF.Silu), reads=[pg], writes=[s_g])
                S.op("dve", lambda e: e.tensor_tensor(out=aT[:, fb, 0:n], in0=s_g[:, 0:n], in1=pu[:, 0:n], op=ALU.mult),
                     reads=[s_g, pu], writes=[aT])
            for ob in range(16):
                sl = w2[ob % 2]
                if w16 is None:
                    S.dma("pool", sl[:].rearrange("p k c -> p (k c)"), self.w2r.t.ap()[widx, ob],
                          reads=[self.wdummy], writes=[sl])
                else:
                    S.dma("sp", sl[:].rearrange("p k c -> p (k c)"), self.w2r16_t.ap()[w16, ob],
                          reads=[self.convBuf], writes=[sl])
                o_ps = po[ob % 2]

                def mmo(e):
                    for (a, w_) in chunks(n):
                        for fc in range(NFB):
                            i = e.matmul(o_ps[:, a:a + w_], lhsT=sl[:, fc, :], rhs=aT[:, fc, a:a + w_], start=(fc == 0), stop=(fc == NFB - 1))
                    return i
                S.op("pe", mmo, reads=[sl, aT], writes=[o_ps])
                S.op("dve", lambda e: e.scalar_tensor_tensor(out=h[:, ob, 0:n], in0=o_ps[:, 0:n], scalar=0.5, in1=h[:, ob, 0:n],
                                                             op0=ALU.mult, op1=ALU.add), reads=[o_ps, h], writes=[h])
            S.dma("sp", dst.t.ap()[:, :, c0:c0 + n].rearrange("k p t -> p k t"), h[:, :, 0:n],
                  reads=[h], writes=[dst])
        S.pop_scope()

    def final_norm(self, src, dst, gcol):
        S = self.S
        S.push_scope()
        h = S.sb("h", [128, 16, 512], F32)
        hn = S.sb("hn", [128, 16, 512], F32)
        sq = S.sb("sq", [128, 16, 512], BF16)
        rs = S.sb("rs", [128, 512], F32)
        pn = S.ps("pn", [128, 512])
        for (c0, n) in GROUPS:
            self.load_h(src, h, c0, n)
            self.norm_group(h, hn, sq, pn, rs, n, gcol)
            S.dma("sp", dst.t.ap()[:, :, c0:c0 + n].rearrange("k p t -> p k t"), hn[:, :, 0:n],
                  reads=[hn], writes=[dst])
        S.pop_scope()


    def mixer_setup(self):
        S = self.S
        IN = "ExternalInput"
        self.wfm = S.dram("wfm", [2, NFM, 128, 16 * 128], F32, IN)
        self.wtm = S.dram("wtm", [2, NTM, 128, 16 * 512], F32, IN)
        self.wbr = S.dram("wbr", [2, 16, 128, 32 * 128], F32, IN)
        self.wor = S.dram("wor", [2, 16, 128, 16 * 128], F32, IN)
        self.pp = S.dram("pp", [128, NPP], F32, IN)
        self.pr = S.dram("pr", [128, NPR], F32, IN)
        self.cst = S.dram("cst", [NCST, 128, 128], F32, IN)
        self.ctab = S.dram("ctab", [NTAB, 128, T], F32, IN)
        dbg = self.cfg.get("dbg", False)
        kind = "ExternalOutput" if dbg else "Internal"
        self.PRF_t = self.nc.dram_tensor("PRF", [NPRF, 128, T], F32, kind=kind)
        self.PRG_t = self.nc.dram_tensor("PRG", [64, 128, T], BF16, kind=kind)
        self.PRV_t = self.nc.dram_tensor("PRV", [6, NB, 128, 512], BF16, kind=kind)
        self.PRM_t = self.nc.dram_tensor("PRM", [NB, 128, 512], F32, kind=kind)
        self.Y_t = self.nc.dram_tensor("Y", [4, 8, 128, T], BF16, kind=kind)
        self.PRF = [Buf("prf%d" % i, self.PRF_t) for i in range(4)]
        self.PRG = [Buf("prg%d" % i, self.PRG_t) for i in range(4)]
        self.PRV = [Buf("prv%d" % i, self.PRV_t) for i in range(2)]
        self.PRM = Buf("prm", self.PRM_t)
        self.Y = [Buf("y%d" % i, self.Y_t) for i in range(4)]
        self.HTm = [Buf("htm%d" % i, self.HT.t) for i in range(4)]
        self.wbr16_t = self.nc.dram_tensor("wbr16", [16, 128, 32 * 128], BF16, kind="Internal")
        self.wor16_t = self.nc.dram_tensor("wor16", [16, 128, 16 * 128], BF16, kind="Internal")
        self.w13r16_t = self.nc.dram_tensor("w13r16", [2, NFB, 128, 16 * 256], BF16, kind="Internal")
        self.w2r16_t = self.nc.dram_tensor("w2r16", [2, 16, 128, NFB * 128], BF16, kind="Internal")
        self.convBuf = Buf("conv")
        self.pending = []
        self.ppt = S.sb("ppt", [128, NPP], F32)
        self.prt = S.sb("prt", [128, NPR], F32)
        S.dma("sp", self.ppt[:], self.pp.t.ap(), reads=[self.pp], writes=[self.ppt])
        S.dma("sp", self.prt[:], self.pr.t.ap(), reads=[self.pr], writes=[self.prt])

    def queue_conversions(self, l):
        S = self.S

        def add(out, in_):
            self.pending.append(lambda: S.dma("pool", out, in_, reads=[], writes=[self.convBuf], nowait=True))
        for ob in range(16):
            add(self.wbr16_t.ap()[ob], self.wbr.t.ap()[l, ob])
            add(self.wor16_t.ap()[ob], self.wor.t.ap()[l, ob])
        jobs = [(0, l * 2 + 1)] + ([(1, 2)] if l == 0 else [])
        for slot, widx in jobs:
            for fb in range(NFB):
                add(self.w13r16_t.ap()[slot, fb], self.w13r.t.ap()[widx, fb])
            for ob in range(16):
                add(self.w2r16_t.ap()[slot, ob], self.w2r.t.ap()[widx, ob])

    def conv_some(self, n):
        while n > 0 and self.pending:
            self.pending.pop(0)()
            n -= 1

    def cload(self, name, idx, dt):
        b = self.S.sb(name, [128, 128], dt)
        self.S.dma("pool", b[:], self.cst.t.ap()[idx], reads=[self.cst], writes=[b])
        return b

    def tload(self, name, idx, dt):
        b = self.S.sb(name, [128, T], dt)
        self.S.dma("pool", b[:], self.ctab.t.ap()[idx], reads=[self.ctab], writes=[b])
        return b

    def proj_phase(self, l):
        S = self.S
        S.push_scope()
        hm = S.sb("hm", [128, 16, T], BF16)
        S.push_scope()
        h = S.sb("h", [128, 16, 512], F32)
        sq = S.sb("sq", [128, 16, 512], BF16)
        rs = S.sb("rs", [128, 512], F32)
        pn = S.ps("pn", [128, 512])
        for (c0, n) in GROUPS:
            self.load_h(self.HT, h, c0, n)
            self.norm_group(h, hm, sq, pn, rs, n, (l * 3 + 1) * 16, pad0=(c0 == 0), ho=c0)
        S.pop_scope()
        slabs = [S.sb("ws%d" % i, [128, 16, 128], BF16) for i in range(4)]
        evf = [S.sb("evf%d" % i, [128, T], F32) for i in range(3)]
        evb = [S.sb("evb%d" % i, [128, T], BF16) for i in range(2)]
        banks = [S.ps("pb%d" % i, [128, 512]) for i in range(6)]
        cnt = 0
        for blk in range(NFM):
            sl = slabs[blk % 4]
            S.dma("pool", sl[:].rearrange("p k c -> p (k c)"), self.wfm.t.ap()[l, blk], reads=[self.wdummy], writes=[sl])
            isg = blk >= NPRF
            ev = evb[blk % 2] if isg else evf[blk % 3]
            for (c0, n) in GROUPS:
                ps = banks[cnt % 6]

                def mm(e):
                    for kc in range(16):
                        i = e.matmul(ps[:, 0:n], lhsT=sl[:, kc, :], rhs=hm[:, kc, c0:c0 + n], start=(kc == 0), stop=(kc == 15))
                    return i
                S.op("pe", mm, reads=[sl, hm], writes=[ps])
                if cnt % 2 == 0:
                    S.op("act", lambda e: e.activation(out=ev[:, c0:c0 + n], in_=ps[:, 0:n], func=AF.Copy), reads=[ps], writes=[ev])
                else:
                    S.op("dve", lambda e: e.tensor_copy(out=ev[:, c0:c0 + n], in_=ps[:, 0:n]), reads=[ps], writes=[ev])
                cnt += 1
            if isg:
                S.dma("sp", self.PRG_t.ap()[blk - NPRF], ev[:], reads=[ev], writes=[self.PRG[blk % 4]])
            else:
                S.dma("sp", self.PRF_t.ap()[blk], ev[:], reads=[ev], writes=[self.PRF[blk % 4]])
        tsl = [S.sb("wt%d" % i, [128, 16, 512], BF16) for i in range(2)]
        tvb = [S.sb("tvb%d" % i, [128, 512], BF16) for i in range(3)]
        tvf = [S.sb("tvf%d" % i, [128, 512], F32) for i in range(2)]
        for s_ in range(NTM):
            sl = tsl[s_ % 2]
            S.dma("pool", sl[:].rearrange("p k c -> p (k c)"), self.wtm.t.ap()[l, s_], reads=[self.wdummy], writes=[sl])
            for t in range(NB):
                ps = banks[cnt % 6]

                def mm(e):
                    for kc in range(16):
                        i = e.matmul(ps[:], lhsT=hm[:, kc, t * 128:(t + 1) * 128], rhs=sl[:, kc, :], start=(kc == 0), stop=(kc == 15))
                    return i
                S.op("pe", mm, reads=[sl, hm], writes=[ps])
                if s_ < 6:
                    ev = tvb[cnt % 3]
                else:
                    ev = tvf[cnt % 2]
                if cnt % 2 == 0:
                    S.op("act", lambda e: e.activation(out=ev[:], in_=ps[:], func=AF.Copy), reads=[ps], writes=[ev])
                else:
                    S.op("dve", lambda e: e.tensor_copy(out=ev[:], in_=ps[:]), reads=[ps], writes=[ev])
                if s_ < 6:
                    S.dma("sp", self.PRV_t.ap()[s_, t], ev[:], reads=[ev], writes=[self.PRV[cnt % 2]])
                else:
                    S.dma("sp", self.PRM_t.ap()[t], ev[:], reads=[ev], writes=[self.PRM])
                cnt += 1
        S.pop_scope()

    def rope_block(self, blk, dst, xf, Rm, cosT, sinT, banks, tmp1, tmp2, dview=None, zsplit=None):
        S = self.S
        S.dma("sp", xf[:], self.PRF_t.ap()[blk], reads=[self.PRF[blk % 4]], writes=[xf])
        for gi, (c0, n) in enumerate(GROUPS):
            ps = banks[gi % len(banks)]
            t1 = tmp1[gi % 2]
            t2 = tmp2[gi % 2]
            S.op("pe", lambda e: e.matmul(ps[:, 0:n], lhsT=Rm[:], rhs=xf[:, c0:c0 + n], start=True, stop=True),
                 reads=[Rm, xf], writes=[ps])
            S.op("pool", lambda e: e.tensor_tensor(out=t1[:, 0:n], in0=xf[:, c0:c0 + n], in1=cosT[:, c0:c0 + n], op=ALU.mult),
                 reads=[xf, cosT], writes=[t1])
            S.op("dve", lambda e: e.tensor_tensor(out=t2[:, 0:n], in0=ps[:, 0:n], in1=sinT[:, c0:c0 + n], op=ALU.mult),
                 reads=[ps, sinT], writes=[t2])
            if zsplit is not None:
                for hf in range(2):
                    zb = zsplit[hf]
                    S.op("pool", lambda e: e.tensor_tensor(out=zb[hf * 64:(hf + 1) * 64, c0:c0 + n], in0=t1[hf * 64:(hf + 1) * 64, 0:n],
                                                           in1=t2[hf * 64:(hf + 1) * 64, 0:n], op=ALU.add), reads=[t1, t2], writes=[zb])
                continue
            dap = dview(c0, n) if dview is not None else dst[:, c0:c0 + n]
            S.op("pool", lambda e: e.tensor_tensor(out=dap, in0=t1[:, 0:n], in1=t2[:, 0:n], op=ALU.add),
                 reads=[t1, t2], writes=[dst])

    def branch_A(self, l):
        S = self.S
        S.push_scope()
        cosA = self.tload("cosA", 0, F32)
        sinA = self.tload("sinA", 1, F32)
        RA = self.cload("RA", C_RA, F32)
        tri = self.cload("tri", C_TRI, BF16)
        mask0 = self.cload("mask0", C_MASK0, BF16)
        ones0 = self.cload("ones0", C_ONES0, BF16)
        vA = S.sb("vA", [128, NB, 1024], BF16)
        for s_ in range(2):
            S.dma("sp", vA[:, :, s_ * 512:(s_ + 1) * 512], self.PRV_t.ap()[s_].rearrange("t p c -> p t c"),
                  reads=self.PRV, writes=[vA])
        lam_init = 0.8 - 0.6 * math.exp(-0.3 * l)
        lt = S.sb("lt", [128, 128], F32)
        ls = S.sb("ls", [128, 2], F32)
        nlam = S.sb("nlam", [128, 1], F32)
        gsub = S.sb("gsub", [128, 1], F32)
        lo = l * PR_L
        S.op("dve", lambda e: e.tensor_tensor(out=lt[:, 0:64], in0=self.prt[:, lo:lo + 64], in1=self.prt[:, lo + 64:lo + 128], op=ALU.mult),
             reads=[self.prt], writes=[lt])
        S.op("dve", lambda e: e.tensor_tensor(out=lt[:, 64:128], in0=self.prt[:, lo + 128:lo + 192], in1=self.prt[:, lo + 192:lo + 256], op=ALU.mult),
             reads=[self.prt, lt], writes=[lt])
        S.op("dve", lambda e: e.tensor_reduce(out=ls[:], in_=lt[:].rearrange("p (a b) -> p a b", a=2), axis=AX.X, op=ALU.add),
             reads=[lt], writes=[ls])
        S.op("act", lambda e: e.activation(out=ls[:], in_=ls[:], func=AF.Exp), reads=[ls], writes=[ls])
        S.op("dve", lambda e: e.tensor_tensor(out=nlam[:], in0=ls[:, 1:2], in1=ls[:, 0:1], op=ALU.subtract), reads=[ls], writes=[nlam])
        S.op("dve", lambda e: e.tensor_scalar(out=nlam[:], in0=nlam[:], scalar1=-lam_init, scalar2=None, op0=ALU.add), reads=[nlam], writes=[nlam])
        gc = l * PP_L + 0
        S.op("dve", lambda e: e.tensor_scalar(out=gsub[:], in0=self.ppt[:, gc:gc + 1], scalar1=1.0 - lam_init, scalar2=None, op0=ALU.mult),
             reads=[self.ppt], writes=[gsub])
        xf = [S.sb("xf%d" % i, [128, T], F32) for i in range(2)]
        qTs = [S.sb("qT%d" % i, [128, T], BF16) for i in range(2)]
        kTs = [[S.sb("kz%d_%d" % (i, c), [128, T], BF16) for c in range(2)] for i in range(2)]
        for i in range(2):
            for c in range(2):
                S.op("pool", lambda e: e.memset(kTs[i][c][:], 0.0), writes=[kTs[i][c]])
        tmp1 = [S.sb("t1_%d" % i, [128, 512], F32) for i in range(2)]
        tmp2 = [S.sb("t2_%d" % i, [128, 512], F32) for i in range(2)]
        pT = [S.sb("pT%d" % i, [128, 512], BF16) for i in range(6)]
        nds = [[S.sb("ndA%d_%d" % (j, i), [128, 512], F32) for i in range(4)] for j in range(2)]
        ngrp = [0]
        deferred = []
        rd = [S.sb("rd%d" % i, [128, 512], F32) for i in range(2)]
        tt = [S.sb("tt%d" % i, [128, 512], F32) for i in range(2)]
        oo = S.sb("oo", [128, 512], F32)
        sqo = S.sb("sqo", [128, 512], BF16)
        rs = S.sb("rsA", [128, 512], F32)
        yb = [S.sb("yb%d" % i, [128, 512], BF16) for i in range(2)]
        B = [S.ps("bA%d" % i, [128, 512]) for i in range(8)]
        self.rope_block(0, qTs[0], xf[0], RA, cosA, sinA, B[0:3], tmp1, tmp2)
        self.rope_block(8, None, xf[1], RA, cosA, sinA, B[0:3], tmp1, tmp2, zsplit=kTs[0])
        cnt = [0]
        for h in range(8):
            self.conv_some(12)
            qT, kT = qTs[h % 2], kTs[h % 2]
            if h + 1 < 8:
                self.rope_block(h + 1, qTs[(h + 1) % 2], xf[0], RA, cosA, sinA, B[0:3], tmp1, tmp2)
                self.rope_block(8 + h + 1, None, xf[1], RA, cosA, sinA, B[0:3], tmp1, tmp2, zsplit=kTs[(h + 1) % 2])
            for gi, (q0, qn) in enumerate(GROUPS):
                nkb = (q0 + qn) // 128
                steps = []
                for i in range(nkb):
                    for c in range(2):
                        steps.append((i, c, cnt[0]))
                        cnt[0] += 1

                def stage1(st):
                    i, c, k = st
                    qlo = max(q0, i * 128)
                    w = q0 + qn - qlo
                    sp_ = B[k % 3]
                    p_ = pT[k % 6]
                    S.op("pe", lambda e: e.matmul(sp_[:, 0:w], lhsT=kT[c][:, i * 128:(i + 1) * 128],
                                                  rhs=qT[:, qlo:qlo + w], start=True, stop=True),
                         reads=[kT[c], qT], writes=[sp_])
                    S.op("act", lambda e: e.activation(out=p_[:, 0:w], in_=sp_[:, 0:w], func=AF.Exp, scale=0.125),
                         reads=[sp_], writes=[p_])
                    if i * 128 >= q0:
                        mk = mask0 if i == 0 else tri
                        S.op("pool", lambda e: e.tensor_tensor(out=p_[:, 0:128], in0=p_[:, 0:128], in1=mk[:], op=ALU.mult),
                             reads=[p_, mk], writes=[p_])

                def stage2(st):
                    i, c, k = st
                    qlo = max(q0, i * 128)
                    w = q0 + qn - qlo
                    off = qlo - q0
                    p_ = pT[k % 6]
                    on = ones0 if i == 0 else self.ones
                    def pv(e):
                        e.matmul(B[4 + c][:, off:off + w], lhsT=vA[:, i, h * 128:(h + 1) * 128], rhs=p_[:, 0:w],
                                 start=(i == 0), stop=(i == nkb - 1))
                        return e.matmul(B[6 + c][:, off:off + w], lhsT=on[:], rhs=p_[:, 0:w],
                                        start=(i == 0), stop=(i == nkb - 1))
                    S.op("pe", pv, reads=[vA, on, p_], writes=[B[4 + c], B[6 + c]])
                ahead = 2
                for k in range(min(ahead, len(steps))):
                    stage1(steps[k])
                for k in range(len(steps)):
                    if k + ahead < len(steps):
                        stage1(steps[k + ahead])
                    stage2(steps[k])
                    if k == min(5, len(steps) - 1) and deferred:
                        deferred.pop()()
                n = qn
                ndv = nds[ngrp[0] % 2]
                ngrp[0] += 1
                for c in range(2):
                    S.op("act", lambda e: e.activation(out=ndv[c][:, 0:n], in_=B[4 + c][:, 0:n], func=AF.Copy), reads=[B[4 + c]], writes=[ndv[c]])
                    S.op("act", lambda e: e.activation(out=ndv[2 + c][:, 0:n], in_=B[6 + c][:, 0:n], func=AF.Copy), reads=[B[6 + c]], writes=[ndv[2 + c]])

                def post(ndv=ndv, n=n, gi=gi, h=h, q0=q0):
                    for c in range(2):
                        S.op("dve", lambda e: e.reciprocal(out=rd[c][:, 0:n], in_=ndv[2 + c][:, 0:n]), reads=[ndv[2 + c]], writes=[rd[c]])
                        S.op("dve", lambda e: e.tensor_tensor(out=tt[c][:, 0:n], in0=ndv[c][:, 0:n], in1=rd[c][:, 0:n], op=ALU.mult),
                             reads=[ndv[c], rd[c]], writes=[tt[c]])
                    S.op("dve", lambda e: e.scalar_tensor_tensor(out=oo[:, 0:n], in0=tt[1][:, 0:n], scalar=nlam[:, 0:1], in1=tt[0][:, 0:n],
                                                                 op0=ALU.mult, op1=ALU.add), reads=[tt[0], tt[1], nlam], writes=[oo])
                    self.headnorm_out(oo, n, sqo, B[3], rs, gsub, yb[gi % 2], None, self.Y_t.ap()[0, h][:, q0:q0 + n], self.Y[0])
                deferred.append(post)
        while deferred:
            deferred.pop()()
        S.pop_scope()

    def headnorm_out(self, oo, n, sqo, ps, rs, gcolap, yb, mul, dst_ap, dst_buf, off=0):
        S = self.S
        S.op("act", lambda e: e.activation(out=sqo[:, 0:n], in_=oo[:, off:off + n], func=AF.Square), reads=[oo], writes=[sqo])
        S.op("pe", lambda e: e.matmul(ps[:, 0:n], lhsT=self.ones[:], rhs=sqo[:, 0:n], start=True, stop=True),
             reads=[sqo, self.ones], writes=[ps])
        S.op("dve", lambda e: e.tensor_scalar(out=rs[:, 0:n], in0=ps[:, 0:n], scalar1=1.0 / 128, scalar2=EPS, op0=ALU.mult, op1=ALU.add),
             reads=[ps], writes=[rs])
        S.op("act", lambda e: e.activation(out=rs[:, 0:n], in_=rs[:, 0:n], func=AF.Ln), reads=[rs], writes=[rs])
        S.op("act", lambda e: e.activation(out=rs[:, 0:n], in_=rs[:, 0:n], func=AF.Exp, scale=-0.5), reads=[rs], writes=[rs])
        if mul is None:
            S.op("dve", lambda e: e.scalar_tensor_tensor(out=yb[:, 0:n], in0=oo[:, off:off + n], scalar=gcolap[:, 0:1], in1=rs[:, 0:n],
                                                         op0=ALU.mult, op1=ALU.mult), reads=[oo, rs, gcolap], writes=[yb])
        else:
            S.op("dve", lambda e: e.scalar_tensor_tensor(out=rs[:, 0:n], in0=oo[:, off:off + n], scalar=gcolap[:, 0:1], in1=rs[:, 0:n],
                                                         op0=ALU.mult, op1=ALU.mult), reads=[oo, rs, gcolap], writes=[rs])
            S.op("dve", lambda e: e.tensor_tensor(out=yb[:, 0:n], in0=rs[:, 0:n], in1=mul[:, off:off + n], op=ALU.mult),
                 reads=[rs, mul], writes=[yb])
        S.dma("sp", dst_ap, yb[:, 0:n], reads=[yb], writes=[dst_buf])


    def gen_B(self, l, Bk, heads=range(8)):
        S = self.S
        ident = self.cload("ident", C_IDENT, F32)
        hgm = self.cload("hgm", C_HGM, BF16)
        rmask = self.tload("rmask", 4, BF16)
        lbT = S.sb("lbT", [128, 8], F32)
        oml = S.sb("oml", [128, 8], F32)
        lo = 2 * PP_L
        if l == 0:
            S.op("dve", lambda e: e.memset(lbT[:], 0.0), writes=[lbT])
        else:
            S.op("dve", lambda e: e.tensor_tensor(out=lbT[:], in0=self.ppt[:, lo + 8:lo + 16], in1=self.ppt[:, lo:lo + 8], op=ALU.subtract),
                 reads=[self.ppt], writes=[lbT])
            S.op("act", lambda e: e.activation(out=lbT[:], in_=lbT[:], func=AF.Sigmoid), reads=[lbT], writes=[lbT])
        S.op("dve", lambda e: e.tensor_scalar(out=oml[:], in0=lbT[:], scalar1=-1.0, scalar2=1.0, op0=ALU.mult, op1=ALU.add),
             reads=[lbT], writes=[oml])
        gcol = S.sb("gcolB", [128, 1], F32)
        gc = l * PP_L + 1
        S.op("dve", lambda e: e.tensor_copy(out=gcol[:], in_=self.ppt[:, gc:gc + 1]), reads=[self.ppt], writes=[gcol])
        Q = S.sb("Q", [128, T], F32)
        L = S.sb("L", [128, T], F32)
        Bf = S.sb("Bf", [128, T], F32)
        X = S.sb("X", [128, T], F32)
        W = S.sb("W", [128, T], F32)
        W2 = S.sb("W2", [128, T], F32)
        G = W2
        oB = Bf
        KE = S.sb("KE", [128, T], F32)
        QE = S.sb("QE", [128, T], BF16)
        QM = S.sb("QM", [128, T], BF16)
        KM = S.sb("KM", [128, T], BF16)
        vB = S.sb("vB", [128, NB, 128], BF16)
        Sf = S.sb("Sf", [128, 128], F32)
        Sb = S.sb("Sb", [128, 128], BF16)
        ATs = [S.sb("ATs%d" % i, [128, 128], BF16) for i in range(2)]
        ket = [[S.sb("ket%d_%d" % (i, c), [128, 128], BF16) for c in range(2)] for i in range(2)]
        for i in range(2):
            for c in range(2):
                S.op("pool", lambda e: e.memset(ket[i][c][:], 0.0), writes=[ket[i][c]])
        sqo = S.sb("sqoB", [128, 512], BF16)
        rs = S.sb("rsB", [128, 512], F32)
        yb = [S.sb("ybB%d" % i, [128, 512], BF16) for i in range(2)]

        def v3(b):
            return b[:].rearrange("p (c k) -> p c k", k=64)
        for h in heads:
            S.dma("sp", Q[:], self.PRF_t.ap()[FM_HGQ + h], reads=self.PRF, writes=[Q])
            S.dma("sp", L[:], self.PRF_t.ap()[FM_HGF + h], reads=self.PRF, writes=[L])
            S.dma("sp", vB[:], self.PRV_t.ap()[2 + h // 4][:, :, (h % 4) * 128:(h % 4 + 1) * 128].rearrange("t p c -> p t c"),
                  reads=self.PRV, writes=[vB])
            S.op("act", lambda e: e.activation(out=X[:], in_=L[:], func=AF.Sigmoid, scale=-1.0), reads=[L], writes=[X])
            S.op("dve", lambda e: e.tensor_scalar(out=X[:], in0=X[:], scalar1=oml[:, h:h + 1], scalar2=None, op0=ALU.mult),
                 reads=[X, oml], writes=[X])
            S.op("act", lambda e: e.activation(out=L[:], in_=L[:], func=AF.Sigmoid), reads=[L], writes=[L])
            S.op("dve", lambda e: e.tensor_scalar(out=L[:], in0=L[:], scalar1=oml[:, h:h + 1], scalar2=lbT[:, h:h + 1],
                                                  op0=ALU.mult, op1=ALU.add), reads=[L, oml, lbT], writes=[L])
            S.op("act", lambda e: e.activation(out=L[:], in_=L[:], func=AF.Ln), reads=[L], writes=[L])
            S.op("dve", lambda e: e.tensor_tensor_scan(out=Bf[:], data0=rmask[:], data1=L[:], initial=0.0, op0=ALU.mult, op1=ALU.add),
                 reads=[rmask, L], writes=[Bf])
            S.op("act", lambda e: e.activation(out=L[:], in_=Bf[:], func=AF.Exp), reads=[Bf], writes=[L])
            S.op("pool", lambda e: e.tensor_tensor(out=QE[:], in0=Q[:], in1=L[:], op=ALU.mult), reads=[Q, L], writes=[QE])
            S.op("dve", lambda e: e.tensor_tensor(out=v3(W), in0=v3(Bf), in1=v3(Bf)[:, :, 31:32].to_broadcast([128, 34, 64]), op=ALU.subtract),
                 reads=[Bf], writes=[W])
            S.op("act", lambda e: e.activation(out=W2[:], in_=W[:], func=AF.Exp), reads=[W], writes=[W2])
            S.op("pool", lambda e: e.tensor_tensor(out=QM[:], in0=Q[:], in1=W2[:], op=ALU.mult), reads=[Q, W2], writes=[QM])
            S.op("act", lambda e: e.activation(out=W2[:], in_=W[:], func=AF.Exp, scale=-1.0), reads=[W], writes=[W2])
            S.op("pool", lambda e: e.tensor_tensor(out=KM[:], in0=X[:], in1=W2[:], op=ALU.mult), reads=[X, W2], writes=[KM])
            S.op("dve", lambda e: e.tensor_tensor(out=v3(W), in0=v3(Bf), in1=v3(Bf)[:, :, 63:64].to_broadcast([128, 34, 64]), op=ALU.subtract),
                 reads=[Bf], writes=[W])
            S.op("act", lambda e: e.activation(out=W2[:], in_=W[:], func=AF.Exp, scale=-1.0), reads=[W], writes=[W2])
            S.op("pool", lambda e: e.tensor_tensor(out=KE[:], in0=X[:], in1=W2[:], op=ALU.mult), reads=[X, W2], writes=[KE])
            S.dma("sp", G[:], self.PRF_t.ap()[FM_HGG + h], reads=self.PRF, writes=[G])
            S.op("act", lambda e: e.activation(out=G[:], in_=G[:], func=AF.Silu), reads=[G], writes=[G])
            yield
            S.op("dve", lambda e: e.memset(Sf[:], 0.0), writes=[Sf])
            S.op("pool", lambda e: e.memset(Sb[:], 0.0), writes=[Sb])
            for t in range(NB):
                c = slice(t * 128, (t + 1) * 128)
                at_ps, kt_ps, o_ps = Bk[0], Bk[1], Bk[2]
                sp = [Bk[3], Bk[3]]
                a_ = ATs[t % 2]
                k_ = ket[t % 2]
                S.op("pe", lambda e: e.matmul(at_ps[:, 0:128], lhsT=KM[:, c], rhs=QM[:, c], start=True, stop=True),
                     reads=[KM, QM], writes=[at_ps])
                S.op("dve", lambda e: e.tensor_tensor(out=a_[:], in0=at_ps[:, 0:128], in1=hgm[:], op=ALU.mult),
                     reads=[at_ps, hgm], writes=[a_])
                S.op("pe", lambda e: e.transpose(kt_ps[:, 0:128], KE[:, c], ident[:]), reads=[KE, ident], writes=[kt_ps])
                for c2 in range(2):
                    S.op("act", lambda e: e.activation(out=k_[c2][c2 * 64:(c2 + 1) * 64, :], in_=kt_ps[c2 * 64:(c2 + 1) * 64, 0:128], func=AF.Copy),
                         reads=[kt_ps], writes=[k_[c2]])
                for cc in range(2):
                    r0 = cc * 64
                    S.op("pe", lambda e: e.matmul(o_ps[:, r0:r0 + 64], lhsT=Sb[:], rhs=QE[:, t * 128 + r0:t * 128 + r0 + 64],
                                                  start=(cc == 0), stop=False, skip_group_check=True), reads=[Sb, QE], writes=[o_ps])
                    if cc == 1:
                        S.op("pe", lambda e: e.matmul(o_ps[:, 0:128], lhsT=vB[:, t, :], rhs=a_[:], start=False, stop=True, skip_group_check=True),
                             reads=[vB, a_], writes=[o_ps])
                    S.op("pe", lambda e: e.matmul(sp[cc][:, 0:128], lhsT=k_[cc][:], rhs=vB[:, t, :], start=True, stop=True),
                         reads=[k_[cc], vB], writes=[sp[cc]])
                    col = t * 128 + r0 + 63
                    S.op("dve", lambda e: e.scalar_tensor_tensor(out=Sf[:], in0=Sf[:], scalar=L[:, col:col + 1], in1=sp[cc][:, 0:128],
                                                                 op0=ALU.mult, op1=ALU.add), reads=[Sf, L, sp[cc]], writes=[Sf])
                    S.op("act", lambda e: e.activation(out=Sb[:], in_=Sf[:], func=AF.Copy), reads=[Sf], writes=[Sb])
                S.op("act", lambda e: e.activation(out=oB[:, c], in_=o_ps[:, 0:128], func=AF.Copy), reads=[o_ps], writes=[oB])
                yield
            for gi, (c0, n) in enumerate(GROUPS):
                self.headnorm_out(oB, n, sqo, Bk[0], rs, gcol, yb[gi % 2], G, self.Y_t.ap()[1, h][:, c0:c0 + n], self.Y[1], off=c0)
            yield

    def gen_D(self, l, Bk, heads=range(4)):
        S = self.S
        ident = self.cload("identD", C_IDENT, F32)
        triF = self.cload("triF", C_TRIF, F32)
        onesF = self.cload("onesF", C_ONESF, F32)
        tri = self.cload("triD", C_TRI, BF16)
        gm = S.sb("gm", [128, NB, 8], F32)
        S.dma("sp", gm[:], self.PRM_t.ap()[:, :, 272:280].rearrange("t p c -> p t c"), reads=[self.PRM], writes=[gm])
        igt = S.sb("igt", [128, NB], F32)
        lft = S.sb("lft", [128, NB], F32)
        bt = S.sb("bt", [128, NB], F32)
        ut = S.sb("ut", [128, NB], F32)
        wt = S.sb("wt", [128, NB], F32)
        lfr = [S.sb("lfr%d" % i, [128, 128], F32) for i in range(2)]
        EBR = S.sb("EBR", [128, T], F32)
        xc1 = S.sb("xc", [128, T + 3], F32)
        xc = [xc1, xc1]
        acc = S.sb("accD", [128, T], F32)
        QS = [S.sb("QS%d" % i, [128, T], BF16) for i in range(2)]
        KF = [S.sb("KF%d" % i, [128, T], F32) for i in range(2)]
        KB = [S.sb("KB%d" % i, [128, T], BF16) for i in range(2)]
        OG = [S.sb("OG%d" % i, [128, T], BF16) for i in range(2)]
        YD = [S.sb("YD%d" % i, [128, T], BF16) for i in range(2)]
        vD = S.sb("vD", [128, NB, 256], BF16)
        Cs = [S.sb("Cs%d" % i, [128, 384], F32) for i in range(2)]
        Cb = [S.sb("Cb%d" % i, [128, 384], BF16) for i in range(2)]
        PTs = [S.sb("PTs%d" % i, [128, 128], BF16) for i in range(2)]
        kw = [S.sb("kw%d" % i, [128, 256], BF16) for i in range(2)]
        rec = [S.sb("rec%d" % i, [128, 128], F32) for i in range(2)]
        hT = [S.sb("hT%d" % i, [128, 128], F32) for i in range(2)]
        S.op("dve", lambda e: e.memset(xc1[:, 0:3], 0.0), writes=[xc1])
        pro = l * PR_L
        for h in heads:
            S.dma("sp", vD[:], self.PRV_t.ap()[4 + h // 2][:, :, (h % 2) * 256:(h % 2 + 1) * 256].rearrange("t p c -> p t c"),
                  reads=self.PRV, writes=[vD])
            S.op("dve", lambda e: e.tensor_scalar(out=igt[:], in0=gm[:, :, h], scalar1=self.prt[:, pro + 256 + h:pro + 257 + h], scalar2=None,
                                                  op0=ALU.add), reads=[gm, self.prt], writes=[igt])
            S.op("dve", lambda e: e.tensor_scalar(out=lft[:], in0=gm[:, :, 4 + h], scalar1=self.prt[:, pro + 260 + h:pro + 261 + h], scalar2=None,
                                                  op0=ALU.add), reads=[gm, self.prt], writes=[lft])
            S.op("act", lambda e: e.activation(out=lft[:], in_=lft[:], func=AF.Sigmoid), reads=[lft], writes=[lft])
            S.op("act", lambda e: e.activation(out=lft[:], in_=lft[:], func=AF.Ln), reads=[lft], writes=[lft])
            S.op("pe", lambda e: e.matmul(Bk[0][:, 0:NB], lhsT=triF[:], rhs=lft[:], start=True, stop=True), reads=[triF, lft], writes=[Bk[0]])
            S.op("pe", lambda e: e.matmul(Bk[1][:, 0:NB], lhsT=onesF[:], rhs=lft[:], start=True, stop=True), reads=[onesF, lft], writes=[Bk[1]])
            S.op("dve", lambda e: e.tensor_tensor(out=ut[:], in0=igt[:], in1=Bk[0][:, 0:NB], op=ALU.subtract), reads=[igt, Bk[0]], writes=[ut])
            S.op("dve", lambda e: e.tensor_tensor(out=wt[:], in0=ut[:], in1=Bk[1][:, 0:NB], op=ALU.add), reads=[ut, Bk[1]], writes=[wt])
            S.op("act", lambda e: e.activation(out=ut[:], in_=ut[:], func=AF.Exp), reads=[ut], writes=[ut])
            S.op("act", lambda e: e.activation(out=wt[:], in_=wt[:], func=AF.Exp), reads=[wt], writes=[wt])
            for t in range(NB):
                lr = lfr[t % 2]
                ps = Bk[2 + (t // 4) % 2]
                S.op("dve", lambda e: e.tensor_scalar(out=lr[:], in0=onesF[:], scalar1=lft[:, t:t + 1], scalar2=None, op0=ALU.mult),
                     reads=[onesF, lft], writes=[lr])
                S.op("pe", lambda e: e.matmul(ps[:, (t % 4) * 128:(t % 4 + 1) * 128], lhsT=lr[:], rhs=triF[:], start=True, stop=True),
                     reads=[lr, triF], writes=[ps])
                if t % 4 == 3 or t == NB - 1:
                    t0 = (t // 4) * 4
                    nn = (t - t0 + 1) * 128
                    S.op("act", lambda e: e.activation(out=EBR[:, t0 * 128:t0 * 128 + nn], in_=ps[:, 0:nn], func=AF.Exp), reads=[ps], writes=[EBR])
            yield
            for which in range(2):
                for dc in range(2):
                    blk16 = which * 8 + h * 2 + dc
                    blk = (FM_MLQ if which == 0 else FM_MLK) + h * 2 + dc
                    x_ = xc[dc]
                    S.dma("sp", x_[:, 3:T + 3], self.PRF_t.ap()[blk], reads=self.PRF, writes=[x_])
                    cb = l * PP_L + 2

                    def wcol(j):
                        return self.ppt[:, cb + j * 16 + blk16:cb + j * 16 + blk16 + 1]
                    S.op("dve", lambda e: e.tensor_scalar(out=acc[:], in0=x_[:, 0:T], scalar1=wcol(0), scalar2=None, op0=ALU.mult),
                         reads=[x_, self.ppt], writes=[acc])
                    for j in range(1, 4):
                        S.op("dve", lambda e: e.scalar_tensor_tensor(out=acc[:], in0=x_[:, j:T + j], scalar=wcol(j), in1=acc[:],
                                                                     op0=ALU.mult, op1=ALU.add), reads=[x_, acc, self.ppt], writes=[acc])
                    yield
                    S.op("act", lambda e: e.activation(out=acc[:], in_=acc[:], func=AF.Silu), reads=[acc], writes=[acc])
                    if which == 0:
                        S.op("pool", lambda e: e.tensor_tensor(out=QS[dc][:], in0=acc[:], in1=EBR[:], op=ALU.mult), reads=[acc, EBR], writes=[QS[dc]])
                    else:
                        S.op("pool", lambda e: e.tensor_scalar(out=KF[dc][:], in0=acc[:], scalar1=0.0625, scalar2=None, op0=ALU.mult),
                             reads=[acc], writes=[KF[dc]])
                        S.op("pool", lambda e: e.tensor_copy(out=KB[dc][:], in_=KF[dc][:]), reads=[KF[dc]], writes=[KB[dc]])
            for dc in range(2):
                S.dma("sp", xc1[:, 3:T + 3], self.PRF_t.ap()[FM_MLO + h * 2 + dc], reads=self.PRF, writes=[xc1])
                S.op("act", lambda e: e.activation(out=OG[dc][:], in_=xc1[:, 3:T + 3], func=AF.Sigmoid), reads=[xc1], writes=[OG[dc]])
                S.op("dve", lambda e: e.memset(Cs[dc][:], 0.0), writes=[Cs[dc]])
                S.op("pool", lambda e: e.memset(Cb[dc][:], 0.0), writes=[Cb[dc]])
            for t in range(NB):
                c = slice(t * 128, (t + 1) * 128)
                pt_ps = Bk[0]
                nd_ps = Bk[1]
                kt_ps = Bk[2]
                p_ = PTs[t % 2]
                k_ = kw[t % 2]

                def mm_s(e):
                    for dc in range(2):
                        i = e.matmul(pt_ps[:, 0:128], lhsT=KB[dc][:, c], rhs=QS[dc][:, c], start=(dc == 0), stop=(dc == 1))
                    return i
                S.op("pe", mm_s, reads=KB + QS, writes=[pt_ps])
                S.op("dve", lambda e: e.scalar_tensor_tensor(out=p_[:], in0=pt_ps[:, 0:128], scalar=ut[:, t:t + 1], in1=tri[:],
                                                             op0=ALU.mult, op1=ALU.mult), reads=[pt_ps, ut, tri], writes=[p_])

                def mm_n(e):
                    for e_ in range(3):
                        lh = vD[:, t, e_ * 128:(e_ + 1) * 128] if e_ < 2 else self.ones[:]
                        e.matmul(nd_ps[:, e_ * 128:(e_ + 1) * 128], lhsT=lh, rhs=p_[:], start=(e_ == 0), stop=False, skip_group_check=True)
                        for dc in range(2):
                            i = e.matmul(nd_ps[:, e_ * 128:(e_ + 1) * 128], lhsT=Cb[dc][:, e_ * 128:(e_ + 1) * 128], rhs=QS[dc][:, c],
                                         start=False, stop=(dc == 1), skip_group_check=True)
                    return i
                S.op("pe", mm_n, reads=[vD, p_, self.ones] + Cb + QS, writes=[nd_ps])
                r_ = rec[t % 2]
                S.op("act", lambda e: e.activation(out=r_[:], in_=nd_ps[:, 256:384], func=AF.Abs), reads=[nd_ps], writes=[r_])
                S.op("dve", lambda e: e.tensor_scalar(out=r_[:], in0=r_[:], scalar1=1.0, scalar2=None, op0=ALU.max),
                     reads=[r_], writes=[r_])
                S.op("dve", lambda e: e.reciprocal(out=r_[:], in_=r_[:]), reads=[r_], writes=[r_])
                for e_ in range(2):
                    h_ = hT[e_]
                    S.op("dve", lambda e: e.tensor_tensor(out=h_[:], in0=nd_ps[:, e_ * 128:(e_ + 1) * 128], in1=r_[:], op=ALU.mult),
                         reads=[nd_ps, r_], writes=[h_])
                    S.op("pool", lambda e: e.tensor_tensor(out=YD[e_][:, c], in0=h_[:], in1=OG[e_][:, c], op=ALU.mult),
                         reads=[h_, OG[e_]], writes=[YD[e_]])
                for dc in range(2):
                    S.op("pe", lambda e: e.transpose(kt_ps[:, dc * 128:(dc + 1) * 128], KF[dc][:, c], ident[:]), reads=[KF[dc], ident], writes=[kt_ps])
                S.op("act", lambda e: e.activation(out=k_[:], in_=kt_ps[:, 0:256], func=AF.Copy, scale=wt[:, t:t + 1]),
                     reads=[kt_ps, wt], writes=[k_])
                for dc in range(2):
                    sp = Bk[3]

                    def mm_c(e):
                        e.matmul(sp[:, 0:256], lhsT=k_[:, dc * 128:(dc + 1) * 128], rhs=vD[:, t, :], start=True, stop=True)
                        return e.matmul(sp[:, 256:384], lhsT=k_[:, dc * 128:(dc + 1) * 128], rhs=self.ones[:], start=False, stop=True, skip_group_check=True)
                    S.op("pe", mm_c, reads=[k_, vD, self.ones], writes=[sp])
                    col = t * 128 + 127
                    S.op("dve", lambda e: e.scalar_tensor_tensor(out=Cs[dc][:], in0=Cs[dc][:], scalar=EBR[:, col:col + 1], in1=sp[:, 0:384],
                                                                 op0=ALU.mult, op1=ALU.add), reads=[Cs[dc], EBR, sp], writes=[Cs[dc]])
                    S.op("act", lambda e: e.activation(out=Cb[dc][:], in_=Cs[dc][:], func=AF.Copy), reads=[Cs[dc]], writes=[Cb[dc]])
                yield
            for e_ in range(2):
                S.dma("sp", self.Y_t.ap()[3, h * 2 + e_], YD[e_][:], reads=[YD[e_]], writes=[self.Y[3]])
            yield

    def branch_pair(self, l, gen, h0, h1):
        S = self.S
        S.push_scope()
        banks = [S.ps("bPR%d" % i, [128, 512]) for i in range(8)]
        t0 = S.record(gen(l, banks[0:4], h0))
        t1 = S.record(gen(l, banks[4:8], h1))
        S.merge([t0, t1])
        S.pop_scope()

    def branch_BD(self, l):
        S = self.S
        S.push_scope()
        banks = [S.ps("bBD%d" % i, [128, 512]) for i in range(8)]
        tb = S.record(self.gen_B(l, banks[0:4]))
        td = S.record(self.gen_D(l, banks[4:8]))
        S.merge([tb, td])
        S.pop_scope()

    def branch_C(self, l):
        S = self.S
        S.push_scope()
        ident = self.cload("identC", C_IDENT, F32)
        negd = self.cload("negd", C_NEGD, F32)
        mask0 = self.cload("mask0C", C_MASK0, BF16)
        ones0 = self.cload("ones0C", C_ONES0, BF16)
        QD = S.sb("QD", [128, 8, T], BF16)
        IQ = S.sb("IQ", [128, 8, T], BF16)
        IKz = [S.sb("IKz%d" % i, [128, T], BF16) for i in range(2)]
        for i in range(2):
            S.op("pool", lambda e: e.memset(IKz[i][:], 0.0), writes=[IKz[i]])
        KD = S.sb("KD", [128, 2, T], BF16)
        vC = S.sb("vC", [128, NB, 256], BF16)
        iw = S.sb("iw", [128, NB, 16], F32)
        Bk = [S.ps("bC%d" % i, [128, 512]) for i in range(8)]
        S.dma("pool", vC[:], self.PRM_t.ap()[:, :, 0:256].rearrange("t p c -> p t c"), reads=[self.PRM], writes=[vC])
        S.dma("sp", iw[:], self.PRM_t.ap()[:, :, 256:272].rearrange("t p c -> p t c"), reads=[self.PRM], writes=[iw])
        S.op("dve", lambda e: e.tensor_scalar(out=iw[:], in0=iw[:], scalar1=1.0 / 32.0, scalar2=None, op0=ALU.mult), reads=[iw], writes=[iw])
        xf = S.sb("xfC", [128, T], F32)
        tmp1 = [S.sb("t1C%d" % i, [128, 512], F32) for i in range(2)]
        tmp2 = [S.sb("t2C%d" % i, [128, 512], F32) for i in range(2)]
        S.push_scope()
        cosS = self.tload("cosS", 2, F32)
        sinS = self.tload("sinS", 3, F32)
        RS = self.cload("RS", C_RS, F32)
        for hd in range(8):
            self.rope_block(FM_DSQ + hd, QD, xf, RS, cosS, sinS, Bk[0:4], tmp1, tmp2, dview=lambda c0, n, hd=hd: QD[:, hd, c0:c0 + n])
        for hd in range(2):
            self.rope_block(FM_DSK + hd, KD, xf, RS, cosS, sinS, Bk[0:4], tmp1, tmp2, dview=lambda c0, n, hd=hd: KD[:, hd, c0:c0 + n])
        S.pop_scope()
        S.push_scope()
        cosA = self.tload("cosA", 0, F32)
        sinA = self.tload("sinA", 1, F32)
        RA = self.cload("RAC", C_RA, F32)
        for hd in range(8):
            self.rope_block(FM_IXQ + hd, IQ, xf, RA, cosA, sinA, Bk[0:4], tmp1, tmp2, dview=lambda c0, n, hd=hd: IQ[:, hd, c0:c0 + n])
        self.rope_block(FM_IXK, None, xf, RA, cosA, sinA, Bk[0:4], tmp1, tmp2, zsplit=IKz)
        S.pop_scope()
        accs = [S.sb("accC%d" % i, [128, 2048], F32) for i in range(2)]
        Wk = S.sb("Wk", [128, 2048], F32)
        sel = S.sb("sel", [128, 2048], F32)
        selTs = [S.sb("selT%d" % i, [128, 16, 128], BF16) for i in range(2)]
        mx = S.sb("mx", [128, 8], F32)
        thr0 = S.sb("thr0", [128, 1], F32)
        S.op("dve", lambda e: e.memset(thr0[:], -1.0e29), writes=[thr0])
        pT = [S.sb("pTC%d" % i, [128, 512], BF16) for i in range(6)]
        rl = [S.sb("rlC%d" % i, [128, 512], F32) for i in range(3)]
        dgs = S.sb("dgs", [128, 16, 128], F32)
        ndc = [S.sb("ndC%d" % i, [128, 512], F32) for i in range(4)]
        rd = S.sb("rdC", [128, 512], F32)
        yC = [S.sb("yC%d" % i, [128, 512], BF16) for i in range(2)]
        cnt = {"ix": 0, "at": 0}
        scale = 128.0 ** -0.5
        IXB = Bk[0:2]
        IACC = Bk[7]
        ATB = [Bk[2], Bk[3], Bk[4]]
        num, den = Bk[5], Bk[6]

        def index_phase(j):
            nk = j * 128
            acc = accs[j % 2]
            for hh in range(16):
                S.op("pool", lambda e: e.tensor_scalar(out=dgs[:, hh, :], in0=ident[:], scalar1=iw[:, j, hh:hh + 1], scalar2=None, op0=ALU.mult),
                     reads=[ident, iw], writes=[dgs])
            for k0 in range(0, nk, 512):
                kn = min(512, nk - k0)

                def sc(hh):
                    hb, half = hh // 2, (hh % 2) * 64
                    ps = IXB[(cnt["ix"] + hh) % 2]
                    r_ = rl[(cnt["ix"] + hh) % 3]
                    S.op("pe", lambda e: e.matmul(ps[:, 0:kn], lhsT=IQ[:, hb, j * 128:(j + 1) * 128],
                                                  rhs=IKz[hh % 2][:, 128 + k0:128 + k0 + kn], start=True, stop=True),
                         reads=[IQ, IKz[hh % 2]], writes=[ps])
                    S.op("act", lambda e: e.activation(out=r_[:, 0:kn], in_=ps[:, 0:kn], func=AF.Relu), reads=[ps], writes=[r_])

                def am(hh):
                    r_ = rl[(cnt["ix"] + hh) % 3]
                    S.op("pe", lambda e: e.matmul(IACC[:, 0:kn], lhsT=dgs[:, hh, :], rhs=r_[:, 0:kn], start=(hh == 0), stop=(hh == 15)),
                         reads=[dgs, r_], writes=[IACC])
                sc(0)
                for hh in range(16):
                    if hh + 1 < 16:
                        sc(hh + 1)
                    am(hh)
                cnt["ix"] += 16
                S.op("act", lambda e: e.activation(out=acc[:, k0:k0 + kn], in_=IACC[:, 0:kn], func=AF.Copy), reads=[IACC], writes=[acc])

        def topk_phase(j):
            nk = j * 128
            acc = accs[j % 2]
            S.op("dve", lambda e: e.tensor_tensor(out=acc[:, nk - 128:nk], in0=acc[:, nk - 128:nk], in1=negd[:], op=ALU.min),
                 reads=[acc, negd], writes=[acc])
            if j >= 3:
                S.op("dve", lambda e: e.tensor_copy(out=Wk[:, 0:nk], in_=acc[:, 0:nk]), reads=[acc], writes=[Wk])
                for r in range(32):
                    S.op("dve", lambda e: e.max(out=mx[:], in_=Wk[:, 0:nk]), reads=[Wk], writes=[mx])
                    if r < 31:
                        S.op("dve", lambda e: e.match_replace(out=Wk[:, 0:nk], in_to_replace=mx[:], in_values=Wk[:, 0:nk], imm_value=-3.0e38),
                             reads=[Wk, mx], writes=[Wk])
                thr = mx[:, 7:8]
                thrb = mx
            else:
                thr = thr0[:, 0:1]
                thrb = thr0
            S.op("dve", lambda e: e.tensor_scalar(out=sel[:, 0:nk], in0=acc[:, 0:nk], scalar1=thr, scalar2=None, op0=ALU.is_ge),
                 reads=[acc, thrb], writes=[sel])

        def index_tr(j):
            selT = selTs[j % 2]
            for kb0 in range(0, j, 4):
                kbn = min(4, j - kb0)
                ps = IXB[cnt["ix"] % 2]
                cnt["ix"] += 1

                def tr(e):
                    for kk in range(kbn):
                        i = e.transpose(ps[:, kk * 128:(kk + 1) * 128], sel[:, (kb0 + kk) * 128:(kb0 + kk + 1) * 128], ident[:])
                    return i
                S.op("pe", tr, reads=[sel, ident], writes=[ps])
                S.op("act", lambda e: e.activation(out=selT[:, kb0:kb0 + kbn, :], in_=ps[:, 0:kbn * 128].rearrange("p (a b) -> p a b", b=128),
                                                   func=AF.Copy), reads=[ps], writes=[selT])

        def attn_phase(j):
            selT = selTs[j % 2]
            for kvh in range(2):
                steps = []
                for i in range(j + 1):
                    steps.append((i, cnt["at"]))
                    cnt["at"] += 1

                def stage1(st):
                    i, k = st
                    sp_ = ATB[k % 3]
                    p_ = pT[k % 6]
                    S.op("pe", lambda e: e.matmul(sp_[:].rearrange("p (a b) -> p a b", b=128), lhsT=KD[:, kvh, i * 128:(i + 1) * 128],
                                                  rhs=QD[:, kvh * 4:(kvh + 1) * 4, j * 128:(j + 1) * 128], start=True, stop=True),
                         reads=[KD, QD], writes=[sp_])
                    S.op("act", lambda e: e.activation(out=p_[:], in_=sp_[:], func=AF.Exp, scale=scale), reads=[sp_], writes=[p_])
                    p3 = p_[:].rearrange("p (a b) -> p a b", b=128)
                    if i == 0 and j == 0:
                        S.op("pool", lambda e: e.tensor_tensor(out=p3, in0=p3, in1=mask0[:].unsqueeze(1).to_broadcast([128, 4, 128]), op=ALU.mult),
                             reads=[p_, mask0], writes=[p_])
                    elif i >= 1:
                        S.op("pool", lambda e: e.tensor_tensor(out=p3, in0=p3, in1=selT[:, i - 1:i, :].to_broadcast([128, 4, 128]), op=ALU.mult),
                             reads=[p_, selT], writes=[p_])

                def stage2(st):
                    i, k = st
                    p_ = pT[k % 6]
                    on = ones0 if i == 0 else self.ones
                    def pv(e):
                        e.matmul(num[:], lhsT=vC[:, i, kvh * 128:(kvh + 1) * 128], rhs=p_[:], start=(i == 0), stop=(i == j))
                        return e.matmul(den[:], lhsT=on[:], rhs=p_[:], start=(i == 0), stop=(i == j))
                    S.op("pe", pv, reads=[vC, on, p_], writes=[num, den])
                ahead = 2
                for k in range(min(ahead, len(steps))):
                    stage1(steps[k])
                for k in range(len(steps)):
                    if k + ahead < len(steps):
                        stage1(steps[k + ahead])
                    stage2(steps[k])
                y_ = yC[kvh]
                n_, d_ = ndc[kvh * 2], ndc[kvh * 2 + 1]
                S.op("act", lambda e: e.activation(out=n_[:], in_=num[:], func=AF.Copy), reads=[num], writes=[n_])
                S.op("act", lambda e: e.activation(out=d_[:], in_=den[:], func=AF.Copy), reads=[den], writes=[d_])
                S.op("dve", lambda e: e.reciprocal(out=rd[:], in_=d_[:]), reads=[d_], writes=[rd])
                S.op("pool", lambda e: e.tensor_tensor(out=y_[:], in0=n_[:], in1=rd[:], op=ALU.mult), reads=[n_, rd], writes=[y_])
                S.dma("sp", self.Y_t.ap()[2, kvh * 4:(kvh + 1) * 4][:, :, j * 128:(j + 1) * 128].rearrange("g p t -> p g t"),
                      y_[:].rearrange("p (a b) -> p a b", b=128), reads=[y_], writes=[self.Y[2]])

        index_phase(1)
        for j in range(NB):
            self.conv_some(6)
            if j + 2 < NB:
                index_phase(j + 2)
            if j + 1 < NB:
                topk_phase(j + 1)
            attn_phase(j)
            if j + 1 < NB:
                index_tr(j + 1)
        S.pop_scope()


    def merge_phase(self, l):
        S = self.S
        S.push_scope()
        NG = 768
        Yg = S.sb("Yg", [128, 32, NG], BF16)
        mT = S.sb("mT", [128, 16, NG], BF16)
        wb = [S.sb("wb%d" % i, [128, 32, 128], BF16) for i in range(3)]
        wo = [S.sb("wo%d" % i, [128, 16, 128], BF16) for i in range(3)]
        gp = [S.sb("gp%d" % i, [128, 4, NG], BF16) for i in range(4)]
        sg = [S.sb("sgm%d" % i, [128, 4, NG], F32) for i in range(2)]
        ma = [S.sb("ma%d" % i, [128, NG], F32) for i in range(2)]
        mb = [S.sb("mb%d" % i, [128, NG], F32) for i in range(2)]
        hb = [S.sb("hb%d" % i, [128, NG], F32) for i in range(4)]
        B = [S.ps("bM%d" % i, [128, 1024]) for i in range(4)]
        nb_ = 0
        self.conv_some(10 ** 6)
        for (c0, n) in GROUPS3:
            for bi in range(4):
                S.dma("sp", Yg[:, bi * 8:(bi + 1) * 8, 0:n], self.Y_t.ap()[bi][:, :, c0:c0 + n].rearrange("r p t -> p r t"),
                      reads=self.Y, writes=[Yg])

            def pre1(ob):
                S.dma("sp", wb[ob % 3][:].rearrange("p k c -> p (k c)"), self.wbr16_t.ap()[ob], reads=[self.convBuf], writes=[wb[ob % 3]])
                S.dma("sp", gp[ob % 4][:, :, 0:n], self.PRG_t.ap()[:, :, c0:c0 + n].rearrange("(b o) p t -> o p b t", o=16)[ob],
                      reads=self.PRG, writes=[gp[ob % 4]])
            pre1(0)
            pre1(1)
            for ob in range(16):
                if ob + 2 < 16:
                    pre1(ob + 2)
                w_ = wb[ob % 3]
                g_ = gp[ob % 4]
                s_ = sg[ob % 2]
                S.op("act", lambda e: e.activation(out=s_[:, :, 0:n], in_=g_[:, :, 0:n], func=AF.Sigmoid), reads=[g_], writes=[s_])
                m_ = ma[ob % 2]
                for bi in range(4):
                    ps = B[nb_ % 3]
                    nb_ += 1

                    def mm(e):
                        for (a, w2_) in chunks(n):
                            for rc in range(8):
                                i = e.matmul(ps[:, a:a + w2_], lhsT=w_[:, bi * 8 + rc, :], rhs=Yg[:, bi * 8 + rc, a:a + w2_], start=(rc == 0), stop=(rc == 7))
                        return i
                    S.op("pe", mm, reads=[w_, Yg], writes=[ps])
                    if bi == 0:
                        S.op("dve", lambda e: e.tensor_tensor(out=m_[:, 0:n], in0=ps[:, 0:n], in1=s_[:, 0, 0:n], op=ALU.mult),
                             reads=[ps, s_], writes=[m_])
                    else:
                        t_ = mb[bi % 2]
                        S.op("dve", lambda e: e.tensor_tensor(out=t_[:, 0:n], in0=ps[:, 0:n], in1=s_[:, bi, 0:n], op=ALU.mult),
                             reads=[ps, s_], writes=[t_])
                        if bi < 3:
                            S.op("pool", lambda e: e.tensor_tensor(out=m_[:, 0:n], in0=m_[:, 0:n], in1=t_[:, 0:n], op=ALU.add),
                                 reads=[m_, t_], writes=[m_])
                        else:
                            S.op("pool", lambda e: e.tensor_tensor(out=mT[:, ob, 0:n], in0=m_[:, 0:n], in1=t_[:, 0:n], op=ALU.add),
                                 reads=[m_, t_], writes=[mT])

            def pre2(ob):
                S.dma("sp", wo[ob % 3][:].rearrange("p k c -> p (k c)"), self.wor16_t.ap()[ob], reads=[self.convBuf], writes=[wo[ob % 3]])
                S.dma("sp", hb[ob % 4][:, 0:n], self.HT.t.ap()[ob, :, c0:c0 + n], reads=[], writes=[hb[ob % 4]])
            pre2(0)
            pre2(1)
            for ob in range(16):
                if ob + 2 < 16:
                    pre2(ob + 2)
                w_ = wo[ob % 3]
                h_ = hb[ob % 4]
                ps = B[3] if ob % 2 == 0 else B[nb_ % 3]
                if ob % 2 == 1:
                    nb_ += 1

                def mm(e):
                    for (a, w2_) in chunks(n):
                        for kc in range(16):
                            i = e.matmul(ps[:, a:a + w2_], lhsT=w_[:, kc, :], rhs=mT[:, kc, a:a + w2_], start=(kc == 0), stop=(kc == 15))
                    return i
                S.op("pe", mm, reads=[w_, mT], writes=[ps])
                S.op("dve", lambda e: e.tensor_tensor(out=h_[:, 0:n], in0=h_[:, 0:n], in1=ps[:, 0:n], op=ALU.add),
                     reads=[ps, h_], writes=[h_])
                S.dma("sp", self.HT.t.ap()[ob, :, c0:c0 + n], h_[:, 0:n], reads=[h_], writes=[self.HTm[ob % 4]])
        S.pop_scope()

    def build(self):
        cfg = self.cfg
        stages = cfg.get("stages", "all")
        if stages == "all":
            self.mixer_setup()
            for l in range(2):
                self.ffn(self.xT if l == 0 else self.HT, self.HT, l * 2, (l * 3) * 16, w16=(None if l == 0 else 1))
                self.proj_phase(l)
                self.queue_conversions(l)
                self.branch_A(l)
                self.branch_pair(l, self.gen_B, [0, 2, 4, 6], [1, 3, 5, 7])
                self.branch_pair(l, self.gen_D, [0, 2], [1, 3])
                self.branch_C(l)
                self.merge_phase(l)
                self.ffn(self.HT, self.HT, l * 2 + 1, (l * 3 + 2) * 16, w16=0)
            self.final_norm(self.HT, self.outT, 96)
        elif stages == "ffn1":
            self.ffn(self.xT, self.outT, 0, 0)
        elif stages == "A":
            self.mixer_setup()
            self.copy_h(self.xT, self.HT)
            self.proj_phase(0)
            self.queue_conversions(0)
            for br in cfg.get("branches", "A"):
                if br == "B":
                    if cfg.get("pair", True):
                        self.branch_pair(0, self.gen_B, [0, 2, 4, 6], [1, 3, 5, 7])
                        self.branch_pair(0, self.gen_D, [0, 2], [1, 3])
                    else:
                        self.branch_BD(0)
                elif br != "D":
                    getattr(self, "branch_" + br)(0)
            if cfg.get("merge", False):
                self.merge_phase(0)
                self.copy_h(self.HT, self.outT)
        self.S.emit()
        return self.nc

    def copy_h(self, src, dst):
        S = self.S
        S.push_scope()
        h = S.sb("hc", [128, 16, 512], F32)
        for (c0, n) in GROUPS:
            self.load_h(src, h, c0, n)
            S.dma("sp", dst.t.ap()[:, :, c0:c0 + n].rearrange("k p t -> p k t"), h[:, :, 0:n], reads=[h], writes=[dst])
        S.pop_scope()


def _fm_cols():
    blocks = []

    def rng(name, off, n=128):
        return list(range(IN_OFF[name] + off, IN_OFF[name] + off + n))
    for h in range(8):
        blocks.append(rng("da_q", h * 128))
    for h in range(8):
        blocks.append(rng("da_k", h * 128))
    for nm in ("hg_q", "hg_f", "hg_g", "ds_q"):
        for h in range(8):
            blocks.append(rng(nm, h * 128))
    for h in range(2):
        blocks.append(rng("ds_k", h * 128))
    for h in range(8):
        blocks.append(rng("ix_q", h * 128))
    blocks.append(rng("ix_k", 0, 64) + rng("ix_k", 0, 64))
    for nm in ("ml_q", "ml_k", "ml_o"):
        for h in range(8):
            blocks.append(rng(nm, h * 128))
    assert len(blocks) == NPRF
    for bi in range(4):
        for ob in range(16):
            blocks.append(rng("gate", bi * 2048 + ob * 128))
    return np.array(blocks, dtype=np.int64)


def _tm_cols():
    sl = []
    for nm in ("da_v", "hg_i", "ml_v"):
        for j in range(2):
            sl.append(list(range(IN_OFF[nm] + j * 512, IN_OFF[nm] + (j + 1) * 512)))
    misc = (list(range(IN_OFF["ds_v"], IN_OFF["ds_v"] + 256)) + list(range(IN_OFF["ix_w"], IN_OFF["ix_w"] + 16))
            + list(range(IN_OFF["ml_i"], IN_OFF["ml_i"] + 4)) + list(range(IN_OFF["ml_f"], IN_OFF["ml_f"] + 4)))
    misc = misc + [IN_OFF["ds_v"]] * (512 - len(misc))
    sl.append(misc)
    return np.array(sl, dtype=np.int64)


def _consts():
    c = np.zeros((NCST, 128, 128), np.float32)
    i = np.arange(128)
    s_, t_ = i[:, None], i[None, :]
    c[C_IDENT] = (s_ == t_)
    c[C_TRIF] = (s_ <= t_)
    c[C_ONESF] = 1.0
    c[C_NEGD] = np.where(t_ <= s_, 3.0e38, -1.0e30)
    c[C_TRI] = (t_ >= s_)
    c[C_MASK0] = ((s_ >= 112) & (s_ <= t_)) | ((t_ < 112) & (s_ == 112))
    c[C_ONES0] = (s_ >= 112) & (t_ >= 0)
    c[C_HGM] = ((s_ // 64) == (t_ // 64)) & (s_ <= t_)
    part = np.arange(128)
    for base in (0, 64):
        for m in range(8):
            part[base + m] = base + m + 8
            part[base + 8 + m] = base + m
    ra = np.zeros((128, 128), np.float32)
    for m in range(128):
        if (m % 64) < 16:
            ra[part[m], m] = 1.0
    c[C_RA] = ra
    rs = np.zeros((128, 128), np.float32)
    for m in range(16):
        rs[m + 16, m] = 1.0
        rs[m, m + 16] = 1.0
    c[C_RS] = rs
    tab = np.zeros((NTAB, 128, T), np.float32)
    pos = np.maximum(np.arange(T) - 112, 0).astype(np.float32)
    inv8 = (np.float32(500000.0) ** (-np.arange(8, dtype=np.float32) / np.float32(8))).astype(np.float32)
    inv16 = (np.float32(500000.0) ** (-np.arange(16, dtype=np.float32) / np.float32(16))).astype(np.float32)
    a8 = (pos[None, :] * inv8[:, None]).astype(np.float32)
    a16 = (pos[None, :] * inv16[:, None]).astype(np.float32)
    tab[0] = 1.0
    tab[2] = 1.0
    for base in (0, 64):
        tab[0, base:base + 8] = np.cos(a8)
        tab[0, base + 8:base + 16] = np.cos(a8)
        tab[1, base:base + 8] = -np.sin(a8)
        tab[1, base + 8:base + 16] = np.sin(a8)
    tab[2, 0:16] = np.cos(a16)
    tab[2, 16:32] = np.cos(a16)
    tab[3, 0:16] = -np.sin(a16)
    tab[3, 16:32] = np.sin(a16)
    tab[4] = (np.arange(T) % 64 != 0)[None, :]
    return c, tab


def _prep_shared(inp):
    sh = {}
    w13 = []
    w2 = []
    for l in range(2):
        for nm in ("ffn1", "ffn2"):
            W = inp[nm + "_w13"][l].reshape(16, 128, 2, NFB, 128)
            w13.append(np.ascontiguousarray(W.transpose(3, 1, 0, 2, 4)).reshape(NFB, 128, 16 * 256))
            W2 = inp[nm + "_w2"][l].reshape(NFB, 128, 16, 128)
            w2.append(np.ascontiguousarray(W2.transpose(2, 1, 0, 3)).reshape(16, 128, NFB * 128))
    sh["w13r"] = np.stack(w13)
    sh["w2r"] = np.stack(w2)
    gl = []
    for l in range(2):
        for nm in ("ffn1_norm", "mix_norm", "ffn2_norm"):
            gl.append(inp[nm][l].reshape(16, 128).T)
    gl.append(inp["final_norm"].reshape(16, 128).T)
    sh["gains"] = np.ascontiguousarray(np.concatenate(gl, axis=1)).astype(np.float32)
    fmc = _fm_cols()
    tmc = _tm_cols()
    wfm = np.empty((2, NFM, 128, 16 * 128), np.float32)
    wtm = np.empty((2, NTM, 128, 16 * 512), np.float32)
    for l in range(2):
        W = inp["w_in"][l]
        g = W[:, fmc.reshape(-1)].reshape(16, 128, NFM, 128)
        wfm[l] = g.transpose(2, 1, 0, 3).reshape(NFM, 128, 16 * 128)
        g = W[:, tmc.reshape(-1)].reshape(16, 128, NTM, 512)
        wtm[l] = g.transpose(2, 1, 0, 3).reshape(NTM, 128, 16 * 512)
    sh["wfm"] = wfm
    sh["wtm"] = wtm
    wb = inp["w_branch"].reshape(2, 4, 8, 128, 16, 128)
    sh["wbr"] = np.ascontiguousarray(wb.transpose(0, 4, 3, 1, 2, 5)).reshape(2, 16, 128, 32 * 128)
    wo = inp["w_out"].reshape(2, 16, 128, 16, 128)
    sh["wor"] = np.ascontiguousarray(wo.transpose(0, 3, 2, 1, 4)).reshape(2, 16, 128, 16 * 128)
    pp = np.zeros((128, NPP), np.float32)
    pr = np.zeros((128, NPR), np.float32)
    for l in range(2):
        o = l * PP_L
        pp[:, o] = inp["da_sub_norm"][l]
        pp[:, o + 1] = inp["hg_norm"][l]
        pp[:, o + 2:o + 66] = inp["ml_conv"][l].reshape(4, 16, 128).transpose(2, 0, 1).reshape(128, 64)
        pp[:, 2 * PP_L + l * 8:2 * PP_L + (l + 1) * 8] = inp["hg_lb_logits"][l].reshape(8, 128).T
        o = l * PR_L
        pr[:, o:o + 256] = inp["da_lambda"][l].reshape(1, 256)
        pr[:, o + 256:o + 260] = inp["ml_i_bias"][l][None, :]
        pr[:, o + 260:o + 264] = inp["ml_f_bias"][l][None, :]
    sh["pp"] = pp
    sh["pr"] = pr
    sh["cst"], sh["ctab"] = _consts()
    return sh


def _prep_core(inp, b):
    h0 = np.zeros((T, D), np.float32)
    h0[112:128] = inp["meta_tokens"]
    h0[128:] = inp["x"][b]
    return {"xT": np.ascontiguousarray(h0.T).reshape(16, 128, T)}


def build_program(cfg=None):
    nc = bass.Bass("TRN2", target_bir_lowering=False)
    k = K(nc, cfg or {})
    return k.build()


def kernel(**inputs):
    inp = {k: np.asarray(v) for k, v in inputs.items()}
    sh = _prep_shared(inp)
    nc = build_program()
    in_maps = []
    for b in range(8):
        m = dict(sh)
        m.update(_prep_core(inp, b))
        in_maps.append(m)
    res = run_bass_kernel_spmd(nc, in_maps, core_ids=list(range(8)))
    out = np.empty((8, 2048, D), np.float32)
    for b in range(8):
        oT = res.results[b]["outT"].reshape(D, T)
        out[b] = oT[:, 128:].T
    return out
```

```python
from contextlib import ExitStack
import math
import numpy as np
import concourse.bass as bass
import concourse.mybir as mybir
from concourse.bass_utils import run_bass_kernel_spmd

F32 = mybir.dt.float32
BF16 = mybir.dt.bfloat16
AF = mybir.ActivationFunctionType
ALU = mybir.AluOpType
AX = mybir.AxisListType

D = 2048
T = 2176
NB = 17
DFF = 5632
NFB = 44
EPS = 1e-6
GROUPS = [(0, 512), (512, 512), (1024, 384), (1408, 384), (1792, 384)]
GROUPS3 = [(0, 768), (768, 768), (1536, 640)]


def chunks(n):
    return [(a, min(512, n - a)) for a in range(0, n, 512)]
IN_GROUPS = (
    ("da_q", 1024), ("da_k", 1024), ("da_v", 1024),
    ("hg_q", 1024), ("hg_f", 1024), ("hg_i", 1024), ("hg_g", 1024),
    ("ds_q", 1024), ("ds_k", 256), ("ds_v", 256),
    ("ix_q", 1024), ("ix_k", 64), ("ix_w", 16),
    ("ml_q", 1024), ("ml_k", 1024), ("ml_v", 1024), ("ml_o", 1024),
    ("ml_i", 4), ("ml_f", 4), ("gate", 8192),
)
IN_OFF = {}
_o = 0
for _n, _w in IN_GROUPS:
    IN_OFF[_n] = _o
    _o += _w
IN_COLS = _o

FM_DAQ, FM_DAK = 0, 8
FM_HGQ, FM_HGF, FM_HGG = 16, 24, 32
FM_DSQ, FM_DSK = 40, 48
FM_IXQ, FM_IXK = 50, 58
FM_MLQ, FM_MLK, FM_MLO = 59, 67, 75
NPRF = 83
NFM = NPRF + 64
NTM = 7
PP_L = 66
NPP = 2 * PP_L + 16
PR_L = 264
NPR = 2 * PR_L
C_IDENT, C_TRIF, C_ONESF, C_NEGD, C_TRI, C_MASK0, C_ONES0, C_HGM, C_RA, C_RS = range(10)
NCST = 10
NTAB = 5


class Buf:
    __slots__ = ("name", "t", "w", "r", "dsem", "dcount")

    def __init__(self, name, t=None):
        self.name = name
        self.t = t
        self.w = None
        self.r = {}
        self.dsem = None
        self.dcount = 0

    def __getitem__(self, idx):
        return self.t[idx]


class _Rec:
    def __init__(self):
        self.calls = []

    def __getattr__(self, name):
        def f(*a, **k):
            self.calls.append((name, a, k))
            return self
        return f


class Sched:
    ENG = ("pe", "act", "dve", "pool", "sp")

    def __init__(self, nc):
        self.nc = nc
        self.es = ExitStack()
        self.scopes = [self.es]
        self.stream = {e: [] for e in self.ENG}
        self.count = {e: 0 for e in self.ENG}
        self.waited = {e: {} for e in self.ENG}
        self.sems = {}
        self.semval = {}
        self.nsem = 0
        for e in self.ENG:
            self.sems[e] = self.es.enter_context(nc.semaphore("s_" + e))
            self.semval[e] = 0
        self.uid = 0
        self.rec = None
        self.scope_bufs = [[]]
        self.free_sems = []

    def push_scope(self):
        es = ExitStack()
        self.scopes.append(es)
        self.scope_bufs.append([])

    def pop_scope(self):
        self.barrier()
        self.scopes.pop().close()
        for b in self.scope_bufs.pop():
            if b.dsem is not None:
                self.free_sems.append(b.dsem)
                b.dsem = None

    def sb(self, name, shape, dt):
        self.uid += 1
        t = self.scopes[-1].enter_context(self.nc.sbuf_tensor("%s_%d" % (name, self.uid), list(shape), dt))
        b = Buf(name, t)
        self.scope_bufs[-1].append(b)
        return b

    def ps(self, name, shape, dt=F32):
        self.uid += 1
        t = self.scopes[-1].enter_context(self.nc.psum_tensor("%s_%d" % (name, self.uid), list(shape), dt))
        return Buf(name, t)

    def dram(self, name, shape, dt, kind="Internal"):
        t = self.nc.dram_tensor(name, list(shape), dt, kind=kind)
        return Buf(name, t)

    def _dsem(self, b):
        if b.dsem is None and self.free_sems:
            b.dsem = self.free_sems.pop()
            b.dcount = self.semval[b.dsem]
        if b.dsem is None:
            self.nsem += 1
            key = "d%d" % self.nsem
            self.sems[key] = self.es.enter_context(self.nc.semaphore(key))
            self.semval[key] = 0
            b.dsem = key
        return b.dsem

    def _deps(self, reads, writes):
        deps = {}

        def add(k, v):
            if deps.get(k, 0) < v:
                deps[k] = v
        for b in reads:
            if b.w is not None:
                add(*b.w)
        for b in writes:
            if b.w is not None:
                add(*b.w)
            for k, v in b.r.items():
                add(k, v)
        return deps

    def _emit_waits(self, eng, deps):
        wd = self.waited[eng]
        for k, v in deps.items():
            if wd.get(k, 0) >= v:
                continue
            wd[k] = v
            sem = self.sems[k]
            self.stream[eng].append(lambda e, sem=sem, v=v: e.wait_ge(sem, v))

    def _commit(self, tok, reads, writes):
        k, v = tok
        for b in reads:
            if b.r.get(k, 0) < v:
                b.r[k] = v
        for b in writes:
            b.w = tok
            b.r = {}

    def op(self, eng, fn, reads=(), writes=()):
        rec = _Rec()
        fn(rec)
        if self.rec is not None:
            self.rec.append(("op", eng, rec.calls, tuple(reads), tuple(writes)))
            return None
        return self._op(eng, rec.calls, reads, writes)

    def _op(self, eng, calls, reads, writes):
        deps = self._deps(reads, writes)
        self._emit_waits(eng, deps)
        self.count[eng] += 1
        self.semval[eng] = self.count[eng]
        tok = (eng, self.count[eng])
        sem = self.sems[eng]

        def play(e, calls=calls, sem=sem):
            for (name, a, k) in calls:
                ins = getattr(e, name)(*a, **k)
            ins.then_inc(sem, 1)
        self.stream[eng].append(play)
        self._commit(tok, reads, writes)
        return tok

    def dma(self, q, out, in_, reads=(), writes=(), **kw):
        if self.rec is not None:
            self.rec.append(("dma", q, out, in_, tuple(reads), tuple(writes), kw))
            return None
        nowait = kw.pop("nowait", False)
        deps = self._deps(reads, writes)
        if not nowait:
            self._emit_waits(q, deps)
        wb = writes[0]
        key = self._dsem(wb)
        wb.dcount += 16
        self.semval[key] = wb.dcount
        tok = (key, wb.dcount)
        sem = self.sems[key]
        self.stream[q].append(
            lambda e, out=out, in_=in_, sem=sem, kw=kw: e.dma_start(out=out, in_=in_, **kw).then_inc(sem, 16))
        self._commit(tok, reads, writes)
        return tok

    def record(self, gen):
        assert self.rec is None
        self.rec = []
        for _ in gen:
            pass
        ops, self.rec = self.rec, None
        return ops

    def merge(self, threads):
        pos = [0] * len(threads)
        total = sum(len(t) for t in threads)
        for _ in range(total):
            best, bf = None, None
            for i, t in enumerate(threads):
                if pos[i] < len(t):
                    f = pos[i] / len(t)
                    if bf is None or f < bf:
                        best, bf = i, f
            o = threads[best][pos[best]]
            pos[best] += 1
            if o[0] == "op":
                self._op(o[1], o[2], o[3], o[4])
            else:
                self.dma(o[1], o[2], o[3], reads=o[4], writes=o[5], **o[6])

    def barrier(self):
        allv = {k: v for k, v in self.semval.items() if v > 0}
        for e in self.ENG:
            self._emit_waits(e, allv)

    def emit(self):
        self.barrier()
        nc = self.nc
        with nc.Block() as block:
            @block.tensor
            def _(e):
                for f in self.stream["pe"]:
                    f(e)

            @block.scalar
            def _(e):
                for f in self.stream["act"]:
                    f(e)

            @block.vector
            def _(e):
                for f in self.stream["dve"]:
                    f(e)

            @block.gpsimd
            def _(e):
                for f in self.stream["pool"]:
                    f(e)

            @block.sync
            def _(e):
                for f in self.stream["sp"]:
                    f(e)
        while self.scopes:
            self.scopes.pop().close()


class K:
    def __init__(self, nc, cfg):
        self.nc = nc
        self.cfg = cfg
        self.S = Sched(nc)
        S = self.S
        IN = "ExternalInput"
        self.xT = S.dram("xT", [16, 128, T], F32, IN)
        self.w13r = S.dram("w13r", [4, NFB, 128, 16 * 256], F32, IN)
        self.w2r = S.dram("w2r", [4, 16, 128, NFB * 128], F32, IN)
        self.gains = S.dram("gains", [128, 112], F32, IN)
        self.outT = S.dram("outT", [16, 128, T], F32, "ExternalOutput")
        self.HT = S.dram("HT", [16, 128, T], F32)
        self.wdummy = Buf("wdummy")
        self.gn = S.sb("gn", [128, 112], F32)
        self.ones = S.sb("ones", [128, 128], BF16)
        S.dma("sp", self.gn[:], self.gains.t.ap(), reads=[self.gains], writes=[self.gn])
        S.op("dve", lambda e: e.memset(self.ones[:], 1.0), writes=[self.ones])

    def load_h(self, src, h, c0, n):
        self.S.dma("sp", h[:, :, 0:n], src.t.ap()[:, :, c0:c0 + n].rearrange("k p t -> p k t"),
                   reads=[src], writes=[h])

    def norm_group(self, h, hn, sq, ps, rs, n, gcol, pad0=False, ho=0):
        S = self.S
        S.op("act", lambda e: e.activation(out=sq[:, 0:16, 0:n], in_=h[:, :, 0:n], func=AF.Square),
             reads=[h], writes=[sq])

        def mm(e):
            for (a, w_) in chunks(n):
                for kc in range(16):
                    i = e.matmul(ps[:, a:a + w_], lhsT=self.ones[:], rhs=sq[:, kc, a:a + w_], start=(kc == 0), stop=(kc == 15))
            return i
        S.op("pe", mm, reads=[sq, self.ones], writes=[ps])
        S.op("dve", lambda e: e.tensor_scalar(out=rs[:, 0:n], in0=ps[:, 0:n], scalar1=1.0 / D, scalar2=EPS,
                                              op0=ALU.mult, op1=ALU.add), reads=[ps], writes=[rs])
        S.op("act", lambda e: e.activation(out=rs[:, 0:n], in_=rs[:, 0:n], func=AF.Ln), reads=[rs], writes=[rs])
        S.op("act", lambda e: e.activation(out=rs[:, 0:n], in_=rs[:, 0:n], func=AF.Exp, scale=-0.5), reads=[rs], writes=[rs])
        if pad0:
            S.op("dve", lambda e: e.memset(rs[:, 0:112], 0.0), reads=[rs], writes=[rs])

        def nrm(e):
            for kc in range(16):
                i = e.scalar_tensor_tensor(out=hn[:, kc, ho:ho + n], in0=h[:, kc, 0:n],
                                           scalar=self.gn[:, gcol + kc:gcol + kc + 1], in1=rs[:, 0:n],
                                           op0=ALU.mult, op1=ALU.mult)
            return i
        S.op("dve", nrm, reads=[h, rs, self.gn], writes=[hn])

    def ffn(self, src, dst, widx, gcol, w16=None):
        S = self.S
        S.push_scope()
        NG = 768
        h = S.sb("h", [128, 16, NG], F32)
        hn = S.sb("hn", [128, 16, NG], BF16)
        aT = S.sb("aT", [128, NFB, NG], BF16)
        rs = S.sb("rs", [128, NG], F32)
        w1 = [S.sb("w1_%d" % i, [128, 16, 256], BF16) for i in range(3)]
        w2 = [S.sb("w2_%d" % i, [128, NFB, 128], BF16) for i in range(2)]
        sg = [S.sb("sg%d" % i, [128, NG], F32) for i in range(2)]
        pg = S.ps("pg", [128, 1024])
        pu = S.ps("pu", [128, 1024])
        po = [S.ps("po%d" % i, [128, 1024]) for i in range(2)]
        nslab = 0
        for (c0, n) in GROUPS3:
            self.load_h(src, h, c0, n)
            self.norm_group(h, hn, aT, po[0], rs, n, gcol)
            for fb in range(NFB):
                sl = w1[nslab % 3]
                nslab += 1
                if w16 is None:
                    S.dma("pool", sl[:].rearrange("p k c -> p (k c)"), self.w13r.t.ap()[widx, fb],
                          reads=[self.wdummy], writes=[sl])
                else:
                    S.dma("sp", sl[:].rearrange("p k c -> p (k c)"), self.w13r16_t.ap()[w16, fb],
                          reads=[self.convBuf], writes=[sl])
                s_g = sg[fb % 2]

                def mmg(e):
                    for (a, w_) in chunks(n):
                        for kc in range(16):
                            i = e.matmul(pg[:, a:a + w_], lhsT=sl[:, kc, 0:128], rhs=hn[:, kc, a:a + w_], start=(kc == 0), stop=(kc == 15))
                    return i

                def mmu(e):
                    for (a, w_) in chunks(n):
                        for kc in range(16):
                            i = e.matmul(pu[:, a:a + w_], lhsT=sl[:, kc, 128:256], rhs=hn[:, kc, a:a + w_], start=(kc == 0), stop=(kc == 15))
                    return i
                S.op("pe", mmg, reads=[sl, hn], writes=[pg])
                S.op("pe", mmu, reads=[sl, hn], writes=[pu])
                S.op("act", lambda e: e.activation(out=s_g[:, 0:n], in_=pg[:, 0:n], func=AF.Silu), reads=[pg], writes=[s_g])
                S.op("dve", lambda e: e.tensor_tensor(out=aT[:, fb, 0:n], in0=s_g[:, 0:n], in1=pu[:, 0:n], op=ALU.mult),
                     reads=[s_g, pu], writes=[aT])
            for ob in range(16):
                sl = w2[ob % 2]
                if w16 is None:
                    S.dma("pool", sl[:].rearrange("p k c -> p (k c)"), self.w2r.t.ap()[widx, ob],
                          reads=[self.wdummy], writes=[sl])
                else:
                    S.dma("sp", sl[:].rearrange("p k c -> p (k c)"), self.w2r16_t.ap()[w16, ob],
                          reads=[self.convBuf], writes=[sl])
                o_ps = po[ob % 2]

                def mmo(e):
                    for (a, w_) in chunks(n):
                        for fc in range(NFB):
                            i = e.matmul(o_ps[:, a:a + w_], lhsT=sl[:, fc, :], rhs=aT[:, fc, a:a + w_], start=(fc == 0), stop=(fc == NFB - 1))
                    return i
                S.op("pe", mmo, reads=[sl, aT], writes=[o_ps])
                S.op("dve", lambda e: e.scalar_tensor_tensor(out=h[:, ob, 0:n], in0=o_ps[:, 0:n], scalar=0.5, in1=h[:, ob, 0:n],
                                                             op0=ALU.mult, op1=ALU.add), reads=[o_ps, h], writes=[h])
            S.dma("sp", dst.t.ap()[:, :, c0:c0 + n].rearrange("k p t -> p k t"), h[:, :, 0:n],
                  reads=[h], writes=[dst])
        S.pop_scope()

    def final_norm(self, src, dst, gcol):
        S = self.S
        S.push_scope()
        h = S.sb("h", [128, 16, 512], F32)
        hn = S.sb("hn", [128, 16, 512], F32)
        sq = S.sb("sq", [128, 16, 512], BF16)
        rs = S.sb("rs", [128, 512], F32)
        pn = S.ps("pn", [128, 512])
        for (c0, n) in GROUPS:
            self.load_h(src, h, c0, n)
            self.norm_group(h, hn, sq, pn, rs, n, gcol)
            S.dma("sp", dst.t.ap()[:, :, c0:c0 + n].rearrange("k p t -> p k t"), hn[:, :, 0:n],
                  reads=[hn], writes=[dst])
        S.pop_scope()


    def mixer_setup(self):
        S = self.S
        IN = "ExternalInput"
        self.wfm = S.dram("wfm", [2, NFM, 128, 16 * 128], F32, IN)
        self.wtm = S.dram("wtm", [2, NTM, 128, 16 * 512], F32, IN)
        self.wbr = S.dram("wbr", [2, 16, 128, 32 * 128], F32, IN)
        self.wor = S.dram("wor", [2, 16, 128, 16 * 128], F32, IN)
        self.pp = S.dram("pp", [128, NPP], F32, IN)
        self.pr = S.dram("pr", [128, NPR], F32, IN)
        self.cst = S.dram("cst", [NCST, 128, 128], F32, IN)
        self.ctab = S.dram("ctab", [NTAB, 128, T], F32, IN)
        dbg = self.cfg.get("dbg", False)
        kind = "ExternalOutput" if dbg else "Internal"
        self.PRF_t = self.nc.dram_tensor("PRF", [NPRF, 128, T], F32, kind=kind)
        self.PRG_t = self.nc.dram_tensor("PRG", [64, 128, T], BF16, kind=kind)
        self.PRV_t = self.nc.dram_tensor("PRV", [6, NB, 128, 512], BF16, kind=kind)
        self.PRM_t = self.nc.dram_tensor("PRM", [NB, 128, 512], F32, kind=kind)
        self.Y_t = self.nc.dram_tensor("Y", [4, 8, 128, T], BF16, kind=kind)
        self.PRF = [Buf("prf%d" % i, self.PRF_t) for i in range(4)]
        self.PRG = [Buf("prg%d" % i, self.PRG_t) for i in range(4)]
        self.PRV = [Buf("prv%d" % i, self.PRV_t) for i in range(2)]
        self.PRM = Buf("prm", self.PRM_t)
        self.Y = [Buf("y%d" % i, self.Y_t) for i in range(4)]
        self.HTm = [Buf("htm%d" % i, self.HT.t) for i in range(4)]
        self.wbr16_t = self.nc.dram_tensor("wbr16", [16, 128, 32 * 128], BF16, kind="Internal")
        self.wor16_t = self.nc.dram_tensor("wor16", [16, 128, 16 * 128], BF16, kind="Internal")
        self.w13r16_t = self.nc.dram_tensor("w13r16", [2, NFB, 128, 16 * 256], BF16, kind="Internal")
        self.w2r16_t = self.nc.dram_tensor("w2r16", [2, 16, 128, NFB * 128], BF16, kind="Internal")
        self.convBuf = Buf("conv")
        self.pending = []
        self.ppt = S.sb("ppt", [128, NPP], F32)
        self.prt = S.sb("prt", [128, NPR], F32)
        S.dma("sp", self.ppt[:], self.pp.t.ap(), reads=[self.pp], writes=[self.ppt])
        S.dma("sp", self.prt[:], self.pr.t.ap(), reads=[self.pr], writes=[self.prt])

    def queue_conversions(self, l):
        S = self.S

        def add(out, in_):
            self.pending.append(lambda: S.dma("pool", out, in_, reads=[], writes=[self.convBuf], nowait=True))
        for ob in range(16):
            add(self.wbr16_t.ap()[ob], self.wbr.t.ap()[l, ob])
            add(self.wor16_t.ap()[ob], self.wor.t.ap()[l, ob])
        jobs = [(0, l * 2 + 1)] + ([(1, 2)] if l == 0 else [])
        for slot, widx in jobs:
            for fb in range(NFB):
                add(self.w13r16_t.ap()[slot, fb], self.w13r.t.ap()[widx, fb])
            for ob in range(16):
                add(self.w2r16_t.ap()[slot, ob], self.w2r.t.ap()[widx, ob])

    def conv_some(self, n):
        while n > 0 and self.pending:
            self.pending.pop(0)()
            n -= 1

    def cload(self, name, idx, dt):
        b = self.S.sb(name, [128, 128], dt)
        self.S.dma("pool", b[:], self.cst.t.ap()[idx], reads=[self.cst], writes=[b])
        return b

    def tload(self, name, idx, dt):
        b = self.S.sb(name, [128, T], dt)
        self.S.dma("pool", b[:], self.ctab.t.ap()[idx], reads=[self.ctab], writes=[b])
        return b

    def proj_phase(self, l):
        S = self.S
        S.push_scope()
        hm = S.sb("hm", [128, 16, T], BF16)
        S.push_scope()
        h = S.sb("h", [128, 16, 512], F32)
        sq = S.sb("sq", [128, 16, 512], BF16)
        rs = S.sb("rs", [128, 512], F32)
        pn = S.ps("pn", [128, 512])
        for (c0, n) in GROUPS:
            self.load_h(self.HT, h, c0, n)
            self.norm_group(h, hm, sq, pn, rs, n, (l * 3 + 1) * 16, pad0=(c0 == 0), ho=c0)
        S.pop_scope()
        slabs = [S.sb("ws%d" % i, [128, 16, 128], BF16) for i in range(4)]
        evf = [S.sb("evf%d" % i, [128, T], F32) for i in range(3)]
        evb = [S.sb("evb%d" % i, [128, T], BF16) for i in range(2)]
        banks = [S.ps("pb%d" % i, [128, 512]) for i in range(6)]
        cnt = 0
        for blk in range(NFM):
            sl = slabs[blk % 4]
            S.dma("pool", sl[:].rearrange("p k c -> p (k c)"), self.wfm.t.ap()[l, blk], reads=[self.wdummy], writes=[sl])
            isg = blk >= NPRF
            ev = evb[blk % 2] if isg else evf[blk % 3]
            for (c0, n) in GROUPS:
                ps = banks[cnt % 6]

                def mm(e):
                    for kc in range(16):
                        i = e.matmul(ps[:, 0:n], lhsT=sl[:, kc, :], rhs=hm[:, kc, c0:c0 + n], start=(kc == 0), stop=(kc == 15))
                    return i
                S.op("pe", mm, reads=[sl, hm], writes=[ps])
                if cnt % 2 == 0:
                    S.op("act", lambda e: e.activation(out=ev[:, c0:c0 + n], in_=ps[:, 0:n], func=AF.Copy), reads=[ps], writes=[ev])
                else:
                    S.op("dve", lambda e: e.tensor_copy(out=ev[:, c0:c0 + n], in_=ps[:, 0:n]), reads=[ps], writes=[ev])
                cnt += 1
            if isg:
                S.dma("sp", self.PRG_t.ap()[blk - NPRF], ev[:], reads=[ev], writes=[self.PRG[blk % 4]])
            else:
                S.dma("sp", self.PRF_t.ap()[blk], ev[:], reads=[ev], writes=[self.PRF[blk % 4]])
        tsl = [S.sb("wt%d" % i, [128, 16, 512], BF16) for i in range(2)]
        tvb = [S.sb("tvb%d" % i, [128, 512], BF16) for i in range(3)]
        tvf = [S.sb("tvf%d" % i, [128, 512], F32) for i in range(2)]
        for s_ in range(NTM):
            sl = tsl[s_ % 2]
            S.dma("pool", sl[:].rearrange("p k c -> p (k c)"), self.wtm.t.ap()[l, s_], reads=[self.wdummy], writes=[sl])
            for t in range(NB):
                ps = banks[cnt % 6]

                def mm(e):
                    for kc in range(16):
                        i = e.matmul(ps[:], lhsT=hm[:, kc, t * 128:(t + 1) * 128], rhs=sl[:, kc, :], start=(kc == 0), stop=(kc == 15))
                    return i
                S.op("pe", mm, reads=[sl, hm], writes=[ps])
                if s_ < 6:
                    ev = tvb[cnt % 3]
                else:
                    ev = tvf[cnt % 2]
                if cnt % 2 == 0:
                    S.op("act", lambda e: e.activation(out=ev[:], in_=ps[:], func=AF.Copy), reads=[ps], writes=[ev])
                else:
                    S.op("dve", lambda e: e.tensor_copy(out=ev[:], in_=ps[:]), reads=[ps], writes=[ev])
                if s_ < 6:
                    S.dma("sp", self.PRV_t.ap()[s_, t], ev[:], reads=[ev], writes=[self.PRV[cnt % 2]])
                else:
                    S.dma("sp", self.PRM_t.ap()[t], ev[:], reads=[ev], writes=[self.PRM])
                cnt += 1
        S.pop_scope()

    def rope_block(self, blk, dst, xf, Rm, cosT, sinT, banks, tmp1, tmp2, dview=None, zsplit=None):
        S = self.S
        S.dma("sp", xf[:], self.PRF_t.ap()[blk], reads=[self.PRF[blk % 4]], writes=[xf])
        for gi, (c0, n) in enumerate(GROUPS):
            ps = banks[gi % len(banks)]
            t1 = tmp1[gi % 2]
            t2 = tmp2[gi % 2]
            S.op("pe", lambda e: e.matmul(ps[:, 0:n], lhsT=Rm[:], rhs=xf[:, c0:c0 + n], start=True, stop=True),
                 reads=[Rm, xf], writes=[ps])
            S.op("pool", lambda e: e.tensor_tensor(out=t1[:, 0:n], in0=xf[:, c0:c0 + n], in1=cosT[:, c0:c0 + n], op=ALU.mult),
                 reads=[xf, cosT], writes=[t1])
            S.op("dve", lambda e: e.tensor_tensor(out=t2[:, 0:n], in0=ps[:, 0:n], in1=sinT[:, c0:c0 + n], op=ALU.mult),
                 reads=[ps, sinT], writes=[t2])
            if zsplit is not None:
                for hf in range(2):
                    zb = zsplit[hf]
                    S.op("pool", lambda e: e.tensor_tensor(out=zb[hf * 64:(hf + 1) * 64, c0:c0 + n], in0=t1[hf * 64:(hf + 1) * 64, 0:n],
                                                           in1=t2[hf * 64:(hf + 1) * 64, 0:n], op=ALU.add), reads=[t1, t2], writes=[zb])
                continue
            dap = dview(c0, n) if dview is not None else dst[:, c0:c0 + n]
            S.op("pool", lambda e: e.tensor_tensor(out=dap, in0=t1[:, 0:n], in1=t2[:, 0:n], op=ALU.add),
                 reads=[t1, t2], writes=[dst])

    def branch_A(self, l):
        S = self.S
        S.push_scope()
        cosA = self.tload("cosA", 0, F32)
        sinA = self.tload("sinA", 1, F32)
        RA = self.cload("RA", C_RA, F32)
        tri = self.cload("tri", C_TRI, BF16)
        mask0 = self.cload("mask0", C_MASK0, BF16)
        ones0 = self.cload("ones0", C_ONES0, BF16)
        vA = S.sb("vA", [128, NB, 1024], BF16)
        for s_ in range(2):
            S.dma("sp", vA[:, :, s_ * 512:(s_ + 1) * 512], self.PRV_t.ap()[s_].rearrange("t p c -> p t c"),
                  reads=self.PRV, writes=[vA])
        lam_init = 0.8 - 0.6 * math.exp(-0.3 * l)
        lt = S.sb("lt", [128, 128], F32)
        ls = S.sb("ls", [128, 2], F32)
        nlam = S.sb("nlam", [128, 1], F32)
        gsub = S.sb("gsub", [128, 1], F32)
        lo = l * PR_L
        S.op("dve", lambda e: e.tensor_tensor(out=lt[:, 0:64], in0=self.prt[:, lo:lo + 64], in1=self.prt[:, lo + 64:lo + 128], op=ALU.mult),
             reads=[self.prt], writes=[lt])
        S.op("dve", lambda e: e.tensor_tensor(out=lt[:, 64:128], in0=self.prt[:, lo + 128:lo + 192], in1=self.prt[:, lo + 192:lo + 256], op=ALU.mult),
             reads=[self.prt, lt], writes=[lt])
        S.op("dve", lambda e: e.tensor_reduce(out=ls[:], in_=lt[:].rearrange("p (a b) -> p a b", a=2), axis=AX.X, op=ALU.add),
             reads=[lt], writes=[ls])
        S.op("act", lambda e: e.activation(out=ls[:], in_=ls[:], func=AF.Exp), reads=[ls], writes=[ls])
        S.op("dve", lambda e: e.tensor_tensor(out=nlam[:], in0=ls[:, 1:2], in1=ls[:, 0:1], op=ALU.subtract), reads=[ls], writes=[nlam])
        S.op("dve", lambda e: e.tensor_scalar(out=nlam[:], in0=nlam[:], scalar1=-lam_init, scalar2=None, op0=ALU.add), reads=[nlam], writes=[nlam])
        gc = l * PP_L + 0
        S.op("dve", lambda e: e.tensor_scalar(out=gsub[:], in0=self.ppt[:, gc:gc + 1], scalar1=1.0 - lam_init, scalar2=None, op0=ALU.mult),
             reads=[self.ppt], writes=[gsub])
        xf = [S.sb("xf%d" % i, [128, T], F32) for i in range(2)]
        qTs = [S.sb("qT%d" % i, [128, T], BF16) for i in range(2)]
        kTs = [[S.sb("kz%d_%d" % (i, c), [128, T], BF16) for c in range(2)] for i in range(2)]
        for i in range(2):
            for c in range(2):
                S.op("pool", lambda e: e.memset(kTs[i][c][:], 0.0), writes=[kTs[i][c]])
        tmp1 = [S.sb("t1_%d" % i, [128, 512], F32) for i in range(2)]
        tmp2 = [S.sb("t2_%d" % i, [128, 512], F32) for i in range(2)]
        pT = [S.sb("pT%d" % i, [128, 512], BF16) for i in range(6)]
        nds = [[S.sb("ndA%d_%d" % (j, i), [128, 512], F32) for i in range(4)] for j in range(2)]
        ngrp = [0]
        deferred = []
        rd = [S.sb("rd%d" % i, [128, 512], F32) for i in range(2)]
        tt = [S.sb("tt%d" % i, [128, 512], F32) for i in range(2)]
        oo = S.sb("oo", [128, 512], F32)
        sqo = S.sb("sqo", [128, 512], BF16)
        rs = S.sb("rsA", [128, 512], F32)
        yb = [S.sb("yb%d" % i, [128, 512], BF16) for i in range(2)]
        B = [S.ps("bA%d" % i, [128, 512]) for i in range(8)]
        self.rope_block(0, qTs[0], xf[0], RA, cosA, sinA, B[0:3], tmp1, tmp2)
        self.rope_block(8, None, xf[1], RA, cosA, sinA, B[0:3], tmp1, tmp2, zsplit=kTs[0])
        cnt = [0]
        for h in range(8):
            self.conv_some(12)
            qT, kT = qTs[h % 2], kTs[h % 2]
            if h + 1 < 8:
                self.rope_block(h + 1, qTs[(h + 1) % 2], xf[0], RA, cosA, sinA, B[0:3], tmp1, tmp2)
                self.rope_block(8 + h + 1, None, xf[1], RA, cosA, sinA, B[0:3], tmp1, tmp2, zsplit=kTs[(h + 1) % 2])
            for gi, (q0, qn) in enumerate(GROUPS):
                nkb = (q0 + qn) // 128
                steps = []
                for i in range(nkb):
                    for c in range(2):
                        steps.append((i, c, cnt[0]))
                        cnt[0] += 1

                def stage1(st):
                    i, c, k = st
                    qlo = max(q0, i * 128)
                    w = q0 + qn - qlo
                    sp_ = B[k % 3]
                    p_ = pT[k % 6]
                    S.op("pe", lambda e: e.matmul(sp_[:, 0:w], lhsT=kT[c][:, i * 128:(i + 1) * 128],
                                                  rhs=qT[:, qlo:qlo + w], start=True, stop=True),
                         reads=[kT[c], qT], writes=[sp_])
                    S.op("act", lambda e: e.activation(out=p_[:, 0:w], in_=sp_[:, 0:w], func=AF.Exp, scale=0.125),
                         reads=[sp_], writes=[p_])
                    if i * 128 >= q0:
                        mk = mask0 if i == 0 else tri
                        S.op("pool", lambda e: e.tensor_tensor(out=p_[:, 0:128], in0=p_[:, 0:128], in1=mk[:], op=ALU.mult),
                             reads=[p_, mk], writes=[p_])

                def stage2(st):
                    i, c, k = st
                    qlo = max(q0, i * 128)
                    w = q0 + qn - qlo
                    off = qlo - q0
                    p_ = pT[k % 6]
                    on = ones0 if i == 0 else self.ones
                    S.op("pe", lambda e: e.matmul(B[4 + c][:, off:off + w], lhsT=vA[:, i, h * 128:(h + 1) * 128], rhs=p_[:, 0:w],
                                                  start=(i == 0), stop=(i == nkb - 1)),
                         reads=[vA, p_], writes=[B[4 + c]])
                    S.op("pe", lambda e: e.matmul(B[6 + c][:, off:off + w], lhsT=on[:], rhs=p_[:, 0:w],
                                                  start=(i == 0), stop=(i == nkb - 1)),
                         reads=[on, p_], writes=[B[6 + c]])
                ahead = 2
                for k in range(min(ahead, len(steps))):
                    stage1(steps[k])
                for k in range(len(steps)):
                    if k + ahead < len(steps):
                        stage1(steps[k + ahead])
                    stage2(steps[k])
                    if k == min(5, len(steps) - 1) and deferred:
                        deferred.pop()()
                n = qn
                ndv = nds[ngrp[0] % 2]
                ngrp[0] += 1
                for c in range(2):
                    S.op("act", lambda e: e.activation(out=ndv[c][:, 0:n], in_=B[4 + c][:, 0:n], func=AF.Copy), reads=[B[4 + c]], writes=[ndv[c]])
                    S.op("act", lambda e: e.activation(out=ndv[2 + c][:, 0:n], in_=B[6 + c][:, 0:n], func=AF.Copy), reads=[B[6 + c]], writes=[ndv[2 + c]])

                def post(ndv=ndv, n=n, gi=gi, h=h, q0=q0):
                    for c in range(2):
                        S.op("dve", lambda e: e.reciprocal(out=rd[c][:, 0:n], in_=ndv[2 + c][:, 0:n]), reads=[ndv[2 + c]], writes=[rd[c]])
                        S.op("dve", lambda e: e.tensor_tensor(out=tt[c][:, 0:n], in0=ndv[c][:, 0:n], in1=rd[c][:, 0:n], op=ALU.mult),
                             reads=[ndv[c], rd[c]], writes=[tt[c]])
                    S.op("dve", lambda e: e.scalar_tensor_tensor(out=oo[:, 0:n], in0=tt[1][:, 0:n], scalar=nlam[:, 0:1], in1=tt[0][:, 0:n],
                                                                 op0=ALU.mult, op1=ALU.add), reads=[tt[0], tt[1], nlam], writes=[oo])
                    self.headnorm_out(oo, n, sqo, B[3], rs, gsub, yb[gi % 2], None, self.Y_t.ap()[0, h][:, q0:q0 + n], self.Y[0])
                deferred.append(post)
        while deferred:
            deferred.pop()()
        S.pop_scope()

    def headnorm_out(self, oo, n, sqo, ps, rs, gcolap, yb, mul, dst_ap, dst_buf, off=0):
        S = self.S
        S.op("act", lambda e: e.activation(out=sqo[:, 0:n], in_=oo[:, off:off + n], func=AF.Square), reads=[oo], writes=[sqo])
        S.op("pe", lambda e: e.matmul(ps[:, 0:n], lhsT=self.ones[:], rhs=sqo[:, 0:n], start=True, stop=True),
             reads=[sqo, self.ones], writes=[ps])
        S.op("dve", lambda e: e.tensor_scalar(out=rs[:, 0:n], in0=ps[:, 0:n], scalar1=1.0 / 128, scalar2=EPS, op0=ALU.mult, op1=ALU.add),
             reads=[ps], writes=[rs])
        S.op("act", lambda e: e.activation(out=rs[:, 0:n], in_=rs[:, 0:n], func=AF.Ln), reads=[rs], writes=[rs])
        S.op("act", lambda e: e.activation(out=rs[:, 0:n], in_=rs[:, 0:n], func=AF.Exp, scale=-0.5), reads=[rs], writes=[rs])
        if mul is None:
            S.op("dve", lambda e: e.scalar_tensor_tensor(out=yb[:, 0:n], in0=oo[:, off:off + n], scalar=gcolap[:, 0:1], in1=rs[:, 0:n],
                                                         op0=ALU.mult, op1=ALU.mult), reads=[oo, rs, gcolap], writes=[yb])
        else:
            S.op("dve", lambda e: e.scalar_tensor_tensor(out=rs[:, 0:n], in0=oo[:, off:off + n], scalar=gcolap[:, 0:1], in1=rs[:, 0:n],
                                                         op0=ALU.mult, op1=ALU.mult), reads=[oo, rs, gcolap], writes=[rs])
            S.op("dve", lambda e: e.tensor_tensor(out=yb[:, 0:n], in0=rs[:, 0:n], in1=mul[:, off:off + n], op=ALU.mult),
                 reads=[rs, mul], writes=[yb])
        S.dma("sp", dst_ap, yb[:, 0:n], reads=[yb], writes=[dst_buf])


    def gen_B(self, l, Bk, heads=range(8)):
        S = self.S
        ident = self.cload("ident", C_IDENT, F32)
        hgm = self.cload("hgm", C_HGM, BF16)
        rmask = self.tload("rmask", 4, BF16)
        lbT = S.sb("lbT", [128, 8], F32)
        oml = S.sb("oml", [128, 8], F32)
        lo = 2 * PP_L
        if l == 0:
            S.op("dve", lambda e: e.memset(lbT[:], 0.0), writes=[lbT])
        else:
            S.op("dve", lambda e: e.tensor_tensor(out=lbT[:], in0=self.ppt[:, lo + 8:lo + 16], in1=self.ppt[:, lo:lo + 8], op=ALU.subtract),
                 reads=[self.ppt], writes=[lbT])
            S.op("act", lambda e: e.activation(out=lbT[:], in_=lbT[:], func=AF.Sigmoid), reads=[lbT], writes=[lbT])
        S.op("dve", lambda e: e.tensor_scalar(out=oml[:], in0=lbT[:], scalar1=-1.0, scalar2=1.0, op0=ALU.mult, op1=ALU.add),
             reads=[lbT], writes=[oml])
        gcol = S.sb("gcolB", [128, 1], F32)
        gc = l * PP_L + 1
        S.op("dve", lambda e: e.tensor_copy(out=gcol[:], in_=self.ppt[:, gc:gc + 1]), reads=[self.ppt], writes=[gcol])
        Q = S.sb("Q", [128, T], F32)
        L = S.sb("L", [128, T], F32)
        Bf = S.sb("Bf", [128, T], F32)
        X = S.sb("X", [128, T], F32)
        W = S.sb("W", [128, T], F32)
        W2 = S.sb("W2", [128, T], F32)
        G = W2
        oB = Bf
        KE = S.sb("KE", [128, T], F32)
        QE = S.sb("QE", [128, T], BF16)
        QM = S.sb("QM", [128, T], BF16)
        KM = S.sb("KM", [128, T], BF16)
        vB = S.sb("vB", [128, NB, 128], BF16)
        Sf = S.sb("Sf", [128, 128], F32)
        Sb = S.sb("Sb", [128, 128], BF16)
        ATs = [S.sb("ATs%d" % i, [128, 128], BF16) for i in range(2)]
        ket = [[S.sb("ket%d_%d" % (i, c), [128, 128], BF16) for c in range(2)] for i in range(2)]
        for i in range(2):
            for c in range(2):
                S.op("pool", lambda e: e.memset(ket[i][c][:], 0.0), writes=[ket[i][c]])
        sqo = S.sb("sqoB", [128, 512], BF16)
        rs = S.sb("rsB", [128, 512], F32)
        yb = [S.sb("ybB%d" % i, [128, 512], BF16) for i in range(2)]

        def v3(b):
            return b[:].rearrange("p (c k) -> p c k", k=64)
        for h in heads:
            S.dma("sp", Q[:], self.PRF_t.ap()[FM_HGQ + h], reads=self.PRF, writes=[Q])
            S.dma("sp", L[:], self.PRF_t.ap()[FM_HGF + h], reads=self.PRF, writes=[L])
            S.dma("sp", vB[:], self.PRV_t.ap()[2 + h // 4][:, :, (h % 4) * 128:(h % 4 + 1) * 128].rearrange("t p c -> p t c"),
                  reads=self.PRV, writes=[vB])
            S.op("act", lambda e: e.activation(out=X[:], in_=L[:], func=AF.Sigmoid, scale=-1.0), reads=[L], writes=[X])
            S.op("dve", lambda e: e.tensor_scalar(out=X[:], in0=X[:], scalar1=oml[:, h:h + 1], scalar2=None, op0=ALU.mult),
                 reads=[X, oml], writes=[X])
            S.op("act", lambda e: e.activation(out=L[:], in_=L[:], func=AF.Sigmoid), reads=[L], writes=[L])
            S.op("dve", lambda e: e.tensor_scalar(out=L[:], in0=L[:], scalar1=oml[:, h:h + 1], scalar2=lbT[:, h:h + 1],
                                                  op0=ALU.mult, op1=ALU.add), reads=[L, oml, lbT], writes=[L])
            S.op("act", lambda e: e.activation(out=L[:], in_=L[:], func=AF.Ln), reads=[L], writes=[L])
            S.op("dve", lambda e: e.tensor_tensor_scan(out=Bf[:], data0=rmask[:], data1=L[:], initial=0.0, op0=ALU.mult, op1=ALU.add),
                 reads=[rmask, L], writes=[Bf])
            S.op("act", lambda e: e.activation(out=L[:], in_=Bf[:], func=AF.Exp), reads=[Bf], writes=[L])
            S.op("pool", lambda e: e.tensor_tensor(out=QE[:], in0=Q[:], in1=L[:], op=ALU.mult), reads=[Q, L], writes=[QE])
            S.op("dve", lambda e: e.tensor_tensor(out=v3(W), in0=v3(Bf), in1=v3(Bf)[:, :, 31:32].to_broadcast([128, 34, 64]), op=ALU.subtract),
                 reads=[Bf], writes=[W])
            S.op("act", lambda e: e.activation(out=W2[:], in_=W[:], func=AF.Exp), reads=[W], writes=[W2])
            S.op("pool", lambda e: e.tensor_tensor(out=QM[:], in0=Q[:], in1=W2[:], op=ALU.mult), reads=[Q, W2], writes=[QM])
            S.op("act", lambda e: e.activation(out=W2[:], in_=W[:], func=AF.Exp, scale=-1.0), reads=[W], writes=[W2])
            S.op("pool", lambda e: e.tensor_tensor(out=KM[:], in0=X[:], in1=W2[:], op=ALU.mult), reads=[X, W2], writes=[KM])
            S.op("dve", lambda e: e.tensor_tensor(out=v3(W), in0=v3(Bf), in1=v3(Bf)[:, :, 63:64].to_broadcast([128, 34, 64]), op=ALU.subtract),
                 reads=[Bf], writes=[W])
            S.op("act", lambda e: e.activation(out=W2[:], in_=W[:], func=AF.Exp, scale=-1.0), reads=[W], writes=[W2])
            S.op("pool", lambda e: e.tensor_tensor(out=KE[:], in0=X[:], in1=W2[:], op=ALU.mult), reads=[X, W2], writes=[KE])
            S.dma("sp", G[:], self.PRF_t.ap()[FM_HGG + h], reads=self.PRF, writes=[G])
            S.op("act", lambda e: e.activation(out=G[:], in_=G[:], func=AF.Silu), reads=[G], writes=[G])
            yield
            S.op("dve", lambda e: e.memset(Sf[:], 0.0), writes=[Sf])
            S.op("pool", lambda e: e.memset(Sb[:], 0.0), writes=[Sb])
            for t in range(NB):
                c = slice(t * 128, (t + 1) * 128)
                at_ps, kt_ps, o_ps = Bk[0], Bk[1], Bk[2]
                sp = [Bk[3], Bk[3]]
                a_ = ATs[t % 2]
                k_ = ket[t % 2]
                S.op("pe", lambda e: e.matmul(at_ps[:, 0:128], lhsT=KM[:, c], rhs=QM[:, c], start=True, stop=True),
                     reads=[KM, QM], writes=[at_ps])
                S.op("dve", lambda e: e.tensor_tensor(out=a_[:], in0=at_ps[:, 0:128], in1=hgm[:], op=ALU.mult),
                     reads=[at_ps, hgm], writes=[a_])
                S.op("pe", lambda e: e.transpose(kt_ps[:, 0:128], KE[:, c], ident[:]), reads=[KE, ident], writes=[kt_ps])
                for c2 in range(2):
                    S.op("act", lambda e: e.activation(out=k_[c2][c2 * 64:(c2 + 1) * 64, :], in_=kt_ps[c2 * 64:(c2 + 1) * 64, 0:128], func=AF.Copy),
                         reads=[kt_ps], writes=[k_[c2]])
                for cc in range(2):
                    r0 = cc * 64
                    S.op("pe", lambda e: e.matmul(o_ps[:, r0:r0 + 64], lhsT=Sb[:], rhs=QE[:, t * 128 + r0:t * 128 + r0 + 64],
                                                  start=(cc == 0), stop=False, skip_group_check=True), reads=[Sb, QE], writes=[o_ps])
                    if cc == 1:
                        S.op("pe", lambda e: e.matmul(o_ps[:, 0:128], lhsT=vB[:, t, :], rhs=a_[:], start=False, stop=True, skip_group_check=True),
                             reads=[vB, a_], writes=[o_ps])
                    S.op("pe", lambda e: e.matmul(sp[cc][:, 0:128], lhsT=k_[cc][:], rhs=vB[:, t, :], start=True, stop=True),
                         reads=[k_[cc], vB], writes=[sp[cc]])
                    col = t * 128 + r0 + 63
                    S.op("dve", lambda e: e.scalar_tensor_tensor(out=Sf[:], in0=Sf[:], scalar=L[:, col:col + 1], in1=sp[cc][:, 0:128],
                                                                 op0=ALU.mult, op1=ALU.add), reads=[Sf, L, sp[cc]], writes=[Sf])
                    S.op("act", lambda e: e.activation(out=Sb[:], in_=Sf[:], func=AF.Copy), reads=[Sf], writes=[Sb])
                S.op("act", lambda e: e.activation(out=oB[:, c], in_=o_ps[:, 0:128], func=AF.Copy), reads=[o_ps], writes=[oB])
                yield
            for gi, (c0, n) in enumerate(GROUPS):
                self.headnorm_out(oB, n, sqo, Bk[0], rs, gcol, yb[gi % 2], G, self.Y_t.ap()[1, h][:, c0:c0 + n], self.Y[1], off=c0)
            yield

    def gen_D(self, l, Bk, heads=range(4)):
        S = self.S
        ident = self.cload("identD", C_IDENT, F32)
        triF = self.cload("triF", C_TRIF, F32)
        onesF = self.cload("onesF", C_ONESF, F32)
        tri = self.cload("triD", C_TRI, BF16)
        gm = S.sb("gm", [128, NB, 8], F32)
        S.dma("sp", gm[:], self.PRM_t.ap()[:, :, 272:280].rearrange("t p c -> p t c"), reads=[self.PRM], writes=[gm])
        igt = S.sb("igt", [128, NB], F32)
        lft = S.sb("lft", [128, NB], F32)
        bt = S.sb("bt", [128, NB], F32)
        ut = S.sb("ut", [128, NB], F32)
        wt = S.sb("wt", [128, NB], F32)
        lfr = [S.sb("lfr%d" % i, [128, 128], F32) for i in range(2)]
        EBR = S.sb("EBR", [128, T], F32)
        xc1 = S.sb("xc", [128, T + 3], F32)
        xc = [xc1, xc1]
        acc = S.sb("accD", [128, T], F32)
        QS = [S.sb("QS%d" % i, [128, T], BF16) for i in range(2)]
        KF = [S.sb("KF%d" % i, [128, T], F32) for i in range(2)]
        KB = [S.sb("KB%d" % i, [128, T], BF16) for i in range(2)]
        OG = [S.sb("OG%d" % i, [128, T], BF16) for i in range(2)]
        YD = [S.sb("YD%d" % i, [128, T], BF16) for i in range(2)]
        vD = S.sb("vD", [128, NB, 256], BF16)
        Cs = [S.sb("Cs%d" % i, [128, 384], F32) for i in range(2)]
        Cb = [S.sb("Cb%d" % i, [128, 384], BF16) for i in range(2)]
        PTs = [S.sb("PTs%d" % i, [128, 128], BF16) for i in range(2)]
        kw = [S.sb("kw%d" % i, [128, 256], BF16) for i in range(2)]
        rec = [S.sb("rec%d" % i, [128, 128], F32) for i in range(2)]
        hT = [S.sb("hT%d" % i, [128, 128], F32) for i in range(2)]
        S.op("dve", lambda e: e.memset(xc1[:, 0:3], 0.0), writes=[xc1])
        pro = l * PR_L
        for h in heads:
            S.dma("sp", vD[:], self.PRV_t.ap()[4 + h // 2][:, :, (h % 2) * 256:(h % 2 + 1) * 256].rearrange("t p c -> p t c"),
                  reads=self.PRV, writes=[vD])
            S.op("dve", lambda e: e.tensor_scalar(out=igt[:], in0=gm[:, :, h], scalar1=self.prt[:, pro + 256 + h:pro + 257 + h], scalar2=None,
                                                  op0=ALU.add), reads=[gm, self.prt], writes=[igt])
            S.op("dve", lambda e: e.tensor_scalar(out=lft[:], in0=gm[:, :, 4 + h], scalar1=self.prt[:, pro + 260 + h:pro + 261 + h], scalar2=None,
                                                  op0=ALU.add), reads=[gm, self.prt], writes=[lft])
            S.op("act", lambda e: e.activation(out=lft[:], in_=lft[:], func=AF.Sigmoid), reads=[lft], writes=[lft])
            S.op("act", lambda e: e.activation(out=lft[:], in_=lft[:], func=AF.Ln), reads=[lft], writes=[lft])
            S.op("pe", lambda e: e.matmul(Bk[0][:, 0:NB], lhsT=triF[:], rhs=lft[:], start=True, stop=True), reads=[triF, lft], writes=[Bk[0]])
            S.op("pe", lambda e: e.matmul(Bk[1][:, 0:NB], lhsT=onesF[:], rhs=lft[:], start=True, stop=True), reads=[onesF, lft], writes=[Bk[1]])
            S.op("dve", lambda e: e.tensor_tensor(out=ut[:], in0=igt[:], in1=Bk[0][:, 0:NB], op=ALU.subtract), reads=[igt, Bk[0]], writes=[ut])
            S.op("dve", lambda e: e.tensor_tensor(out=wt[:], in0=ut[:], in1=Bk[1][:, 0:NB], op=ALU.add), reads=[ut, Bk[1]], writes=[wt])
            S.op("act", lambda e: e.activation(out=ut[:], in_=ut[:], func=AF.Exp), reads=[ut], writes=[ut])
            S.op("act", lambda e: e.activation(out=wt[:], in_=wt[:], func=AF.Exp), reads=[wt], writes=[wt])
            for t in range(NB):
                lr = lfr[t % 2]
                ps = Bk[2 + (t // 4) % 2]
                S.op("dve", lambda e: e.tensor_scalar(out=lr[:], in0=onesF[:], scalar1=lft[:, t:t + 1], scalar2=None, op0=ALU.mult),
                     reads=[onesF, lft], writes=[lr])
                S.op("pe", lambda e: e.matmul(ps[:, (t % 4) * 128:(t % 4 + 1) * 128], lhsT=lr[:], rhs=triF[:], start=True, stop=True),
                     reads=[lr, triF], writes=[ps])
                if t % 4 == 3 or t == NB - 1:
                    t0 = (t // 4) * 4
                    nn = (t - t0 + 1) * 128
                    S.op("act", lambda e: e.activation(out=EBR[:, t0 * 128:t0 * 128 + nn], in_=ps[:, 0:nn], func=AF.Exp), reads=[ps], writes=[EBR])
            yield
            for which in range(2):
                for dc in range(2):
                    blk16 = which * 8 + h * 2 + dc
                    blk = (FM_MLQ if which == 0 else FM_MLK) + h * 2 + dc
                    x_ = xc[dc]
                    S.dma("sp", x_[:, 3:T + 3], self.PRF_t.ap()[blk], reads=self.PRF, writes=[x_])
                    cb = l * PP_L + 2

                    def wcol(j):
                        return self.ppt[:, cb + j * 16 + blk16:cb + j * 16 + blk16 + 1]
                    S.op("dve", lambda e: e.tensor_scalar(out=acc[:], in0=x_[:, 0:T], scalar1=wcol(0), scalar2=None, op0=ALU.mult),
                         reads=[x_, self.ppt], writes=[acc])
                    for j in range(1, 4):
                        S.op("dve", lambda e: e.scalar_tensor_tensor(out=acc[:], in0=x_[:, j:T + j], scalar=wcol(j), in1=acc[:],
                                                                     op0=ALU.mult, op1=ALU.add), reads=[x_, acc, self.ppt], writes=[acc])
                    yield
                    S.op("act", lambda e: e.activation(out=acc[:], in_=acc[:], func=AF.Silu), reads=[acc], writes=[acc])
                    if which == 0:
                        S.op("pool", lambda e: e.tensor_tensor(out=QS[dc][:], in0=acc[:], in1=EBR[:], op=ALU.mult), reads=[acc, EBR], writes=[QS[dc]])
                    else:
                        S.op("pool", lambda e: e.tensor_scalar(out=KF[dc][:], in0=acc[:], scalar1=0.0625, scalar2=None, op0=ALU.mult),
                             reads=[acc], writes=[KF[dc]])
                        S.op("pool", lambda e: e.tensor_copy(out=KB[dc][:], in_=KF[dc][:]), reads=[KF[dc]], writes=[KB[dc]])
            for dc in range(2):
                S.dma("sp", xc1[:, 3:T + 3], self.PRF_t.ap()[FM_MLO + h * 2 + dc], reads=self.PRF, writes=[xc1])
                S.op("act", lambda e: e.activation(out=OG[dc][:], in_=xc1[:, 3:T + 3], func=AF.Sigmoid), reads=[xc1], writes=[OG[dc]])
                S.op("dve", lambda e: e.memset(Cs[dc][:], 0.0), writes=[Cs[dc]])
                S.op("pool", lambda e: e.memset(Cb[dc][:], 0.0), writes=[Cb[dc]])
            for t in range(NB):
                c = slice(t * 128, (t + 1) * 128)
                pt_ps = Bk[0]
                nd_ps = Bk[1]
                kt_ps = Bk[2]
                p_ = PTs[t % 2]
                k_ = kw[t % 2]

                def mm_s(e):
                    for dc in range(2):
                        i = e.matmul(pt_ps[:, 0:128], lhsT=KB[dc][:, c], rhs=QS[dc][:, c], start=(dc == 0), stop=(dc == 1))
                    return i
                S.op("pe", mm_s, reads=KB + QS, writes=[pt_ps])
                S.op("dve", lambda e: e.scalar_tensor_tensor(out=p_[:], in0=pt_ps[:, 0:128], scalar=ut[:, t:t + 1], in1=tri[:],
                                                             op0=ALU.mult, op1=ALU.mult), reads=[pt_ps, ut, tri], writes=[p_])

                def mm_n(e):
                    for e_ in range(3):
                        lh = vD[:, t, e_ * 128:(e_ + 1) * 128] if e_ < 2 else self.ones[:]
                        e.matmul(nd_ps[:, e_ * 128:(e_ + 1) * 128], lhsT=lh, rhs=p_[:], start=(e_ == 0), stop=False, skip_group_check=True)
                        for dc in range(2):
                            i = e.matmul(nd_ps[:, e_ * 128:(e_ + 1) * 128], lhsT=Cb[dc][:, e_ * 128:(e_ + 1) * 128], rhs=QS[dc][:, c],
                                         start=False, stop=(dc == 1), skip_group_check=True)
                    return i
                S.op("pe", mm_n, reads=[vD, p_, self.ones] + Cb + QS, writes=[nd_ps])
                r_ = rec[t % 2]
                S.op("act", lambda e: e.activation(out=r_[:], in_=nd_ps[:, 256:384], func=AF.Abs), reads=[nd_ps], writes=[r_])
                S.op("dve", lambda e: e.tensor_scalar(out=r_[:], in0=r_[:], scalar1=1.0, scalar2=None, op0=ALU.max),
                     reads=[r_], writes=[r_])
                S.op("dve", lambda e: e.reciprocal(out=r_[:], in_=r_[:]), reads=[r_], writes=[r_])
                for e_ in range(2):
                    h_ = hT[e_]
                    S.op("dve", lambda e: e.tensor_tensor(out=h_[:], in0=nd_ps[:, e_ * 128:(e_ + 1) * 128], in1=r_[:], op=ALU.mult),
                         reads=[nd_ps, r_], writes=[h_])
                    S.op("pool", lambda e: e.tensor_tensor(out=YD[e_][:, c], in0=h_[:], in1=OG[e_][:, c], op=ALU.mult),
                         reads=[h_, OG[e_]], writes=[YD[e_]])
                for dc in range(2):
                    S.op("pe", lambda e: e.transpose(kt_ps[:, dc * 128:(dc + 1) * 128], KF[dc][:, c], ident[:]), reads=[KF[dc], ident], writes=[kt_ps])
                S.op("act", lambda e: e.activation(out=k_[:], in_=kt_ps[:, 0:256], func=AF.Copy, scale=wt[:, t:t + 1]),
                     reads=[kt_ps, wt], writes=[k_])
                for dc in range(2):
                    sp = Bk[3]

                    def mm_c(e):
                        e.matmul(sp[:, 0:256], lhsT=k_[:, dc * 128:(dc + 1) * 128], rhs=vD[:, t, :], start=True, stop=True)
                        return e.matmul(sp[:, 256:384], lhsT=k_[:, dc * 128:(dc + 1) * 128], rhs=self.ones[:], start=False, stop=True, skip_group_check=True)
                    S.op("pe", mm_c, reads=[k_, vD, self.ones], writes=[sp])
                    col = t * 128 + 127
                    S.op("dve", lambda e: e.scalar_tensor_tensor(out=Cs[dc][:], in0=Cs[dc][:], scalar=EBR[:, col:col + 1], in1=sp[:, 0:384],
                                                                 op0=ALU.mult, op1=ALU.add), reads=[Cs[dc], EBR, sp], writes=[Cs[dc]])
                    S.op("act", lambda e: e.activation(out=Cb[dc][:], in_=Cs[dc][:], func=AF.Copy), reads=[Cs[dc]], writes=[Cb[dc]])
                yield
            for e_ in range(2):
                S.dma("sp", self.Y_t.ap()[3, h * 2 + e_], YD[e_][:], reads=[YD[e_]], writes=[self.Y[3]])
            yield

    def branch_pair(self, l, gen, h0, h1):
        S = self.S
        S.push_scope()
        banks = [S.ps("bPR%d" % i, [128, 512]) for i in range(8)]
        t0 = S.record(gen(l, banks[0:4], h0))
        t1 = S.record(gen(l, banks[4:8], h1))
        S.merge([t0, t1])
        S.pop_scope()

    def branch_BD(self, l):
        S = self.S
        S.push_scope()
        banks = [S.ps("bBD%d" % i, [128, 512]) for i in range(8)]
        tb = S.record(self.gen_B(l, banks[0:4]))
        td = S.record(self.gen_D(l, banks[4:8]))
        S.merge([tb, td])
        S.pop_scope()

    def branch_C(self, l):
        S = self.S
        S.push_scope()
        ident = self.cload("identC", C_IDENT, F32)
        negd = self.cload("negd", C_NEGD, F32)
        mask0 = self.cload("mask0C", C_MASK0, BF16)
        ones0 = self.cload("ones0C", C_ONES0, BF16)
        QD = S.sb("QD", [128, 8, T], BF16)
        IQ = S.sb("IQ", [128, 8, T], BF16)
        IKz = [S.sb("IKz%d" % i, [128, T], BF16) for i in range(2)]
        for i in range(2):
            S.op("pool", lambda e: e.memset(IKz[i][:], 0.0), writes=[IKz[i]])
        KD = S.sb("KD", [128, 2, T], BF16)
        vC = S.sb("vC", [128, NB, 256], BF16)
        iw = S.sb("iw", [128, NB, 16], F32)
        Bk = [S.ps("bC%d" % i, [128, 512]) for i in range(8)]
        S.dma("pool", vC[:], self.PRM_t.ap()[:, :, 0:256].rearrange("t p c -> p t c"), reads=[self.PRM], writes=[vC])
        S.dma("sp", iw[:], self.PRM_t.ap()[:, :, 256:272].rearrange("t p c -> p t c"), reads=[self.PRM], writes=[iw])
        S.op("dve", lambda e: e.tensor_scalar(out=iw[:], in0=iw[:], scalar1=1.0 / 32.0, scalar2=None, op0=ALU.mult), reads=[iw], writes=[iw])
        xf = S.sb("xfC", [128, T], F32)
        tmp1 = [S.sb("t1C%d" % i, [128, 512], F32) for i in range(2)]
        tmp2 = [S.sb("t2C%d" % i, [128, 512], F32) for i in range(2)]
        S.push_scope()
        cosS = self.tload("cosS", 2, F32)
        sinS = self.tload("sinS", 3, F32)
        RS = self.cload("RS", C_RS, F32)
        for hd in range(8):
            self.rope_block(FM_DSQ + hd, QD, xf, RS, cosS, sinS, Bk[0:4], tmp1, tmp2, dview=lambda c0, n, hd=hd: QD[:, hd, c0:c0 + n])
        for hd in range(2):
            self.rope_block(FM_DSK + hd, KD, xf, RS, cosS, sinS, Bk[0:4], tmp1, tmp2, dview=lambda c0, n, hd=hd: KD[:, hd, c0:c0 + n])
        S.pop_scope()
        S.push_scope()
        cosA = self.tload("cosA", 0, F32)
        sinA = self.tload("sinA", 1, F32)
        RA = self.cload("RAC", C_RA, F32)
        for hd in range(8):
            self.rope_block(FM_IXQ + hd, IQ, xf, RA, cosA, sinA, Bk[0:4], tmp1, tmp2, dview=lambda c0, n, hd=hd: IQ[:, hd, c0:c0 + n])
        self.rope_block(FM_IXK, None, xf, RA, cosA, sinA, Bk[0:4], tmp1, tmp2, zsplit=IKz)
        S.pop_scope()
        accs = [S.sb("accC%d" % i, [128, 2048], F32) for i in range(2)]
        Wk = S.sb("Wk", [128, 2048], F32)
        sel = S.sb("sel", [128, 2048], F32)
        selTs = [S.sb("selT%d" % i, [128, 16, 128], BF16) for i in range(2)]
        mx = S.sb("mx", [128, 8], F32)
        thr0 = S.sb("thr0", [128, 1], F32)
        S.op("dve", lambda e: e.memset(thr0[:], -1.0e29), writes=[thr0])
        pT = [S.sb("pTC%d" % i, [128, 512], BF16) for i in range(6)]
        rl = [S.sb("rlC%d" % i, [128, 512], F32) for i in range(3)]
        dgs = S.sb("dgs", [128, 16, 128], F32)
        ndc = [S.sb("ndC%d" % i, [128, 512], F32) for i in range(4)]
        rd = S.sb("rdC", [128, 512], F32)
        yC = [S.sb("yC%d" % i, [128, 512], BF16) for i in range(2)]
        cnt = {"ix": 0, "at": 0}
        scale = 128.0 ** -0.5
        IXB = Bk[0:2]
        IACC = Bk[7]
        ATB = [Bk[2], Bk[3], Bk[4]]
        num, den = Bk[5], Bk[6]

        def index_phase(j):
            nk = j * 128
            acc = accs[j % 2]
            for hh in range(16):
                S.op("pool", lambda e: e.tensor_scalar(out=dgs[:, hh, :], in0=ident[:], scalar1=iw[:, j, hh:hh + 1], scalar2=None, op0=ALU.mult),
                     reads=[ident, iw], writes=[dgs])
            for k0 in range(0, nk, 512):
                kn = min(512, nk - k0)

                def sc(hh):
                    hb, half = hh // 2, (hh % 2) * 64
                    ps = IXB[(cnt["ix"] + hh) % 2]
                    r_ = rl[(cnt["ix"] + hh) % 3]
                    S.op("pe", lambda e: e.matmul(ps[:, 0:kn], lhsT=IQ[:, hb, j * 128:(j + 1) * 128],
                                                  rhs=IKz[hh % 2][:, 128 + k0:128 + k0 + kn], start=True, stop=True),
                         reads=[IQ, IKz[hh % 2]], writes=[ps])
                    S.op("act", lambda e: e.activation(out=r_[:, 0:kn], in_=ps[:, 0:kn], func=AF.Relu), reads=[ps], writes=[r_])

                def am(hh):
                    r_ = rl[(cnt["ix"] + hh) % 3]
                    S.op("pe", lambda e: e.matmul(IACC[:, 0:kn], lhsT=dgs[:, hh, :], rhs=r_[:, 0:kn], start=(hh == 0), stop=(hh == 15)),
                         reads=[dgs, r_], writes=[IACC])
                sc(0)
                for hh in range(16):
                    if hh + 1 < 16:
                        sc(hh + 1)
                    am(hh)
                cnt["ix"] += 16
                S.op("act", lambda e: e.activation(out=acc[:, k0:k0 + kn], in_=IACC[:, 0:kn], func=AF.Copy), reads=[IACC], writes=[acc])

        def topk_phase(j):
            nk = j * 128
            acc = accs[j % 2]
            S.op("dve", lambda e: e.tensor_tensor(out=acc[:, nk - 128:nk], in0=acc[:, nk - 128:nk], in1=negd[:], op=ALU.min),
                 reads=[acc, negd], writes=[acc])
            if j >= 3:
                S.op("dve", lambda e: e.tensor_copy(out=Wk[:, 0:nk], in_=acc[:, 0:nk]), reads=[acc], writes=[Wk])
                for r in range(32):
                    S.op("dve", lambda e: e.max(out=mx[:], in_=Wk[:, 0:nk]), reads=[Wk], writes=[mx])
                    if r < 31:
                        S.op("dve", lambda e: e.match_replace(out=Wk[:, 0:nk], in_to_replace=mx[:], in_values=Wk[:, 0:nk], imm_value=-3.0e38),
                             reads=[Wk, mx], writes=[Wk])
                thr = mx[:, 7:8]
                thrb = mx
            else:
                thr = thr0[:, 0:1]
                thrb = thr0
            S.op("dve", lambda e: e.tensor_scalar(out=sel[:, 0:nk], in0=acc[:, 0:nk], scalar1=thr, scalar2=None, op0=ALU.is_ge),
                 reads=[acc, thrb], writes=[sel])

        def index_tr(j):
            selT = selTs[j % 2]
            for kb0 in range(0, j, 4):
                kbn = min(4, j - kb0)
                ps = IXB[cnt["ix"] % 2]
                cnt["ix"] += 1

                def tr(e):
                    for kk in range(kbn):
                        i = e.transpose(ps[:, kk * 128:(kk + 1) * 128], sel[:, (kb0 + kk) * 128:(kb0 + kk + 1) * 128], ident[:])
                    return i
                S.op("pe", tr, reads=[sel, ident], writes=[ps])
                S.op("act", lambda e: e.activation(out=selT[:, kb0:kb0 + kbn, :], in_=ps[:, 0:kbn * 128].rearrange("p (a b) -> p a b", b=128),
                                                   func=AF.Copy), reads=[ps], writes=[selT])

        def attn_phase(j):
            selT = selTs[j % 2]
            for kvh in range(2):
                steps = []
                for i in range(j + 1):
                    steps.append((i, cnt["at"]))
                    cnt["at"] += 1

                def stage1(st):
                    i, k = st
                    sp_ = ATB[k % 3]
                    p_ = pT[k % 6]
                    S.op("pe", lambda e: e.matmul(sp_[:].rearrange("p (a b) -> p a b", b=128), lhsT=KD[:, kvh, i * 128:(i + 1) * 128],
                                                  rhs=QD[:, kvh * 4:(kvh + 1) * 4, j * 128:(j + 1) * 128], start=True, stop=True),
                         reads=[KD, QD], writes=[sp_])
                    S.op("act", lambda e: e.activation(out=p_[:], in_=sp_[:], func=AF.Exp, scale=scale), reads=[sp_], writes=[p_])
                    p3 = p_[:].rearrange("p (a b) -> p a b", b=128)
                    if i == 0 and j == 0:
                        S.op("pool", lambda e: e.tensor_tensor(out=p3, in0=p3, in1=mask0[:].unsqueeze(1).to_broadcast([128, 4, 128]), op=ALU.mult),
                             reads=[p_, mask0], writes=[p_])
                    elif i >= 1:
                        S.op("pool", lambda e: e.tensor_tensor(out=p3, in0=p3, in1=selT[:, i - 1:i, :].to_broadcast([128, 4, 128]), op=ALU.mult),
                             reads=[p_, selT], writes=[p_])

                def stage2(st):
                    i, k = st
                    p_ = pT[k % 6]
                    on = ones0 if i == 0 else self.ones
                    S.op("pe", lambda e: e.matmul(num[:], lhsT=vC[:, i, kvh * 128:(kvh + 1) * 128], rhs=p_[:], start=(i == 0), stop=(i == j)),
                         reads=[vC, p_], writes=[num])
                    S.op("pe", lambda e: e.matmul(den[:], lhsT=on[:], rhs=p_[:], start=(i == 0), stop=(i == j)),
                         reads=[on, p_], writes=[den])
                ahead = 2
                for k in range(min(ahead, len(steps))):
                    stage1(steps[k])
                for k in range(len(steps)):
                    if k + ahead < len(steps):
                        stage1(steps[k + ahead])
                    stage2(steps[k])
                y_ = yC[kvh]
                n_, d_ = ndc[kvh * 2], ndc[kvh * 2 + 1]
                S.op("act", lambda e: e.activation(out=n_[:], in_=num[:], func=AF.Copy), reads=[num], writes=[n_])
                S.op("act", lambda e: e.activation(out=d_[:], in_=den[:], func=AF.Ln), reads=[den], writes=[d_])
                S.op("act", lambda e: e.activation(out=rd[:], in_=d_[:], func=AF.Exp, scale=-1.0), reads=[d_], writes=[rd])
                S.op("pool", lambda e: e.tensor_tensor(out=y_[:], in0=n_[:], in1=rd[:], op=ALU.mult), reads=[n_, rd], writes=[y_])
                S.dma("sp", self.Y_t.ap()[2, kvh * 4:(kvh + 1) * 4][:, :, j * 128:(j + 1) * 128].rearrange("g p t -> p g t"),
                      y_[:].rearrange("p (a b) -> p a b", b=128), reads=[y_], writes=[self.Y[2]])

        index_phase(1)
        for j in range(NB):
            self.conv_some(6)
            if j + 2 < NB:
                index_phase(j + 2)
            if j + 1 < NB:
                topk_phase(j + 1)
            attn_phase(j)
            if j + 1 < NB:
                index_tr(j + 1)
        S.pop_scope()


    def merge_phase(self, l):
        S = self.S
        S.push_scope()
        NG = 768
        Yg = S.sb("Yg", [128, 32, NG], BF16)
        mT = S.sb("mT", [128, 16, NG], BF16)
        wb = [S.sb("wb%d" % i, [128, 32, 128], BF16) for i in range(3)]
        wo = [S.sb("wo%d" % i, [128, 16, 128], BF16) for i in range(3)]
        gp = [S.sb("gp%d" % i, [128, 4, NG], BF16) for i in range(4)]
        sg = [S.sb("sgm%d" % i, [128, 4, NG], F32) for i in range(2)]
        ma = [S.sb("ma%d" % i, [128, NG], F32) for i in range(2)]
        mb = [S.sb("mb%d" % i, [128, NG], F32) for i in range(2)]
        hb = [S.sb("hb%d" % i, [128, NG], F32) for i in range(4)]
        B = [S.ps("bM%d" % i, [128, 1024]) for i in range(4)]
        nb_ = 0
        self.conv_some(10 ** 6)
        for (c0, n) in GROUPS3:
            for bi in range(4):
                S.dma("sp", Yg[:, bi * 8:(bi + 1) * 8, 0:n], self.Y_t.ap()[bi][:, :, c0:c0 + n].rearrange("r p t -> p r t"),
                      reads=self.Y, writes=[Yg])

            def pre1(ob):
                S.dma("sp", wb[ob % 3][:].rearrange("p k c -> p (k c)"), self.wbr16_t.ap()[ob], reads=[self.convBuf], writes=[wb[ob % 3]])
                S.dma("sp", gp[ob % 4][:, :, 0:n], self.PRG_t.ap()[:, :, c0:c0 + n].rearrange("(b o) p t -> o p b t", o=16)[ob],
                      reads=self.PRG, writes=[gp[ob % 4]])
            pre1(0)
            pre1(1)
            for ob in range(16):
                if ob + 2 < 16:
                    pre1(ob + 2)
                w_ = wb[ob % 3]
                g_ = gp[ob % 4]
                s_ = sg[ob % 2]
                S.op("act", lambda e: e.activation(out=s_[:, :, 0:n], in_=g_[:, :, 0:n], func=AF.Sigmoid), reads=[g_], writes=[s_])
                m_ = ma[ob % 2]
                for bi in range(4):
                    ps = B[nb_ % 3]
                    nb_ += 1

                    def mm(e):
                        for (a, w2_) in chunks(n):
                            for rc in range(8):
                                i = e.matmul(ps[:, a:a + w2_], lhsT=w_[:, bi * 8 + rc, :], rhs=Yg[:, bi * 8 + rc, a:a + w2_], start=(rc == 0), stop=(rc == 7))
                        return i
                    S.op("pe", mm, reads=[w_, Yg], writes=[ps])
                    if bi == 0:
                        S.op("dve", lambda e: e.tensor_tensor(out=m_[:, 0:n], in0=ps[:, 0:n], in1=s_[:, 0, 0:n], op=ALU.mult),
                             reads=[ps, s_], writes=[m_])
                    else:
                        t_ = mb[bi % 2]
                        S.op("dve", lambda e: e.tensor_tensor(out=t_[:, 0:n], in0=ps[:, 0:n], in1=s_[:, bi, 0:n], op=ALU.mult),
                             reads=[ps, s_], writes=[t_])
                        if bi < 3:
                            S.op("pool", lambda e: e.tensor_tensor(out=m_[:, 0:n], in0=m_[:, 0:n], in1=t_[:, 0:n], op=ALU.add),
                                 reads=[m_, t_], writes=[m_])
                        else:
                            S.op("pool", lambda e: e.tensor_tensor(out=mT[:, ob, 0:n], in0=m_[:, 0:n], in1=t_[:, 0:n], op=ALU.add),
                                 reads=[m_, t_], writes=[mT])

            def pre2(ob):
                S.dma("sp", wo[ob % 3][:].rearrange("p k c -> p (k c)"), self.wor16_t.ap()[ob], reads=[self.convBuf], writes=[wo[ob % 3]])
                S.dma("sp", hb[ob % 4][:, 0:n], self.HT.t.ap()[ob, :, c0:c0 + n], reads=[], writes=[hb[ob % 4]])
            pre2(0)
            pre2(1)
            for ob in range(16):
                if ob + 2 < 16:
                    pre2(ob + 2)
                w_ = wo[ob % 3]
                h_ = hb[ob % 4]
                ps = B[3] if ob % 2 == 0 else B[nb_ % 3]
                if ob % 2 == 1:
                    nb_ += 1

                def mm(e):
                    for (a, w2_) in chunks(n):
                        for kc in range(16):
                            i = e.matmul(ps[:, a:a + w2_], lhsT=w_[:, kc, :], rhs=mT[:, kc, a:a + w2_], start=(kc == 0), stop=(kc == 15))
                    return i
                S.op("pe", mm, reads=[w_, mT], writes=[ps])
                S.op("dve", lambda e: e.tensor_tensor(out=h_[:, 0:n], in0=h_[:, 0:n], in1=ps[:, 0:n], op=ALU.add),
                     reads=[ps, h_], writes=[h_])
                S.dma("sp", self.HT.t.ap()[ob, :, c0:c0 + n], h_[:, 0:n], reads=[h_], writes=[self.HTm[ob % 4]])
        S.pop_scope()

    def build(self):
        cfg = self.cfg
        stages = cfg.get("stages", "all")
        if stages == "all":
            self.mixer_setup()
            for l in range(2):
                self.ffn(self.xT if l == 0 else self.HT, self.HT, l * 2, (l * 3) * 16, w16=(None if l == 0 else 1))
                self.proj_phase(l)
                self.queue_conversions(l)
                self.branch_A(l)
                self.branch_pair(l, self.gen_B, [0, 2, 4, 6], [1, 3, 5, 7])
                self.branch_pair(l, self.gen_D, [0, 2], [1, 3])
                self.branch_C(l)
                self.merge_phase(l)
                self.ffn(self.HT, self.HT, l * 2 + 1, (l * 3 + 2) * 16, w16=0)
            self.final_norm(self.HT, self.outT, 96)
        elif stages == "ffn1":
            self.ffn(self.xT, self.outT, 0, 0)
        elif stages == "A":
            self.mixer_setup()
            self.copy_h(self.xT, self.HT)
            self.proj_phase(0)
            self.queue_conversions(0)
            for br in cfg.get("branches", "A"):
                if br == "B":
                    if cfg.get("pair", True):
                        self.branch_pair(0, self.gen_B, [0, 2, 4, 6], [1, 3, 5, 7])
                        self.branch_pair(0, self.gen_D, [0, 2], [1, 3])
                    else:
                        self.branch_BD(0)
                elif br != "D":
                    getattr(self, "branch_" + br)(0)
            if cfg.get("merge", False):
                self.merge_phase(0)
                self.copy_h(self.HT, self.outT)
        self.S.emit()
        return self.nc

    def copy_h(self, src, dst):
        S = self.S
        S.push_scope()
        h = S.sb("hc", [128, 16, 512], F32)
        for (c0, n) in GROUPS:
            self.load_h(src, h, c0, n)
            S.dma("sp", dst.t.ap()[:, :, c0:c0 + n].rearrange("k p t -> p k t"), h[:, :, 0:n], reads=[h], writes=[dst])
        S.pop_scope()


def _fm_cols():
    blocks = []

    def rng(name, off, n=128):
        return list(range(IN_OFF[name] + off, IN_OFF[name] + off + n))
    for h in range(8):
        blocks.append(rng("da_q", h * 128))
    for h in range(8):
        blocks.append(rng("da_k", h * 128))
    for nm in ("hg_q", "hg_f", "hg_g", "ds_q"):
        for h in range(8):
            blocks.append(rng(nm, h * 128))
    for h in range(2):
        blocks.append(rng("ds_k", h * 128))
    for h in range(8):
        blocks.append(rng("ix_q", h * 128))
    blocks.append(rng("ix_k", 0, 64) + rng("ix_k", 0, 64))
    for nm in ("ml_q", "ml_k", "ml_o"):
        for h in range(8):
            blocks.append(rng(nm, h * 128))
    assert len(blocks) == NPRF
    for bi in range(4):
        for ob in range(16):
            blocks.append(rng("gate", bi * 2048 + ob * 128))
    return np.array(blocks, dtype=np.int64)


def _tm_cols():
    sl = []
    for nm in ("da_v", "hg_i", "ml_v"):
        for j in range(2):
            sl.append(list(range(IN_OFF[nm] + j * 512, IN_OFF[nm] + (j + 1) * 512)))
    misc = (list(range(IN_OFF["ds_v"], IN_OFF["ds_v"] + 256)) + list(range(IN_OFF["ix_w"], IN_OFF["ix_w"] + 16))
            + list(range(IN_OFF["ml_i"], IN_OFF["ml_i"] + 4)) + list(range(IN_OFF["ml_f"], IN_OFF["ml_f"] + 4)))
    misc = misc + [IN_OFF["ds_v"]] * (512 - len(misc))
    sl.append(misc)
    return np.array(sl, dtype=np.int64)


def _consts():
    c = np.zeros((NCST, 128, 128), np.float32)
    i = np.arange(128)
    s_, t_ = i[:, None], i[None, :]
    c[C_IDENT] = (s_ == t_)
    c[C_TRIF] = (s_ <= t_)
    c[C_ONESF] = 1.0
    c[C_NEGD] = np.where(t_ <= s_, 3.0e38, -1.0e30)
    c[C_TRI] = (t_ >= s_)
    c[C_MASK0] = ((s_ >= 112) & (s_ <= t_)) | ((t_ < 112) & (s_ == 112))
    c[C_ONES0] = (s_ >= 112) & (t_ >= 0)
    c[C_HGM] = ((s_ // 64) == (t_ // 64)) & (s_ <= t_)
    part = np.arange(128)
    for base in (0, 64):
        for m in range(8):
            part[base + m] = base + m + 8
            part[base + 8 + m] = base + m
    ra = np.zeros((128, 128), np.float32)
    for m in range(128):
        if (m % 64) < 16:
            ra[part[m], m] = 1.0
    c[C_RA] = ra
    rs = np.zeros((128, 128), np.float32)
    for m in range(16):
        rs[m + 16, m] = 1.0
        rs[m, m + 16] = 1.0
    c[C_RS] = rs
    tab = np.zeros((NTAB, 128, T), np.float32)
    pos = np.maximum(np.arange(T) - 112, 0).astype(np.float32)
    inv8 = (np.float32(500000.0) ** (-np.arange(8, dtype=np.float32) / np.float32(8))).astype(np.float32)
    inv16 = (np.float32(500000.0) ** (-np.arange(16, dtype=np.float32) / np.float32(16))).astype(np.float32)
    a8 = (pos[None, :] * inv8[:, None]).astype(np.float32)
    a16 = (pos[None, :] * inv16[:, None]).astype(np.float32)
    tab[0] = 1.0
    tab[2] = 1.0
    for base in (0, 64):
        tab[0, base:base + 8] = np.cos(a8)
        tab[0, base + 8:base + 16] = np.cos(a8)
        tab[1, base:base + 8] = -np.sin(a8)
        tab[1, base + 8:base + 16] = np.sin(a8)
    tab[2, 0:16] = np.cos(a16)
    tab[2, 16:32] = np.cos(a16)
    tab[3, 0:16] = -np.sin(a16)
    tab[3, 16:32] = np.sin(a16)
    tab[4] = (np.arange(T) % 64 != 0)[None, :]
    return c, tab


def _prep_shared(inp):
    sh = {}
    w13 = []
    w2 = []
    for l in range(2):
        for nm in ("ffn1", "ffn2"):
            W = inp[nm + "_w13"][l].reshape(16, 128, 2, NFB, 128)
            w13.append(np.ascontiguousarray(W.transpose(3, 1, 0, 2, 4)).reshape(NFB, 128, 16 * 256))
            W2 = inp[nm + "_w2"][l].reshape(NFB, 128, 16, 128)
            w2.append(np.ascontiguousarray(W2.transpose(2, 1, 0, 3)).reshape(16, 128, NFB * 128))
    sh["w13r"] = np.stack(w13)
    sh["w2r"] = np.stack(w2)
    gl = []
    for l in range(2):
        for nm in ("ffn1_norm", "mix_norm", "ffn2_norm"):
            gl.append(inp[nm][l].reshape(16, 128).T)
    gl.append(inp["final_norm"].reshape(16, 128).T)
    sh["gains"] = np.ascontiguousarray(np.concatenate(gl, axis=1)).astype(np.float32)
    fmc = _fm_cols()
    tmc = _tm_cols()
    wfm = np.empty((2, NFM, 128, 16 * 128), np.float32)
    wtm = np.empty((2, NTM, 128, 16 * 512), np.float32)
    for l in range(2):
        W = inp["w_in"][l]
        g = W[:, fmc.reshape(-1)].reshape(16, 128, NFM, 128)
        wfm[l] = g.transpose(2, 1, 0, 3).reshape(NFM, 128, 16 * 128)
        g = W[:, tmc.reshape(-1)].reshape(16, 128, NTM, 512)
        wtm[l] = g.transpose(2, 1, 0, 3).reshape(NTM, 128, 16 * 512)
    sh["wfm"] = wfm
    sh["wtm"] = wtm
    wb = inp["w_branch"].reshape(2, 4, 8, 128, 16, 128)
    sh["wbr"] = np.ascontiguousarray(wb.transpose(0, 4, 3, 1, 2, 5)).reshape(2, 16, 128, 32 * 128)
    wo = inp["w_out"].reshape(2, 16, 128, 16, 128)
    sh["wor"] = np.ascontiguousarray(wo.transpose(0, 3, 2, 1, 4)).reshape(2, 16, 128, 16 * 128)
    pp = np.zeros((128, NPP), np.float32)
    pr = np.zeros((128, NPR), np.float32)
    for l in range(2):
        o = l * PP_L
        pp[:, o] = inp["da_sub_norm"][l]
        pp[:, o + 1] = inp["hg_norm"][l]
        pp[:, o + 2:o + 66] = inp["ml_conv"][l].reshape(4, 16, 128).transpose(2, 0, 1).reshape(128, 64)
        pp[:, 2 * PP_L + l * 8:2 * PP_L + (l + 1) * 8] = inp["hg_lb_logits"][l].reshape(8, 128).T
        o = l * PR_L
        pr[:, o:o + 256] = inp["da_lambda"][l].reshape(1, 256)
        pr[:, o + 256:o + 260] = inp["ml_i_bias"][l][None, :]
        pr[:, o + 260:o + 264] = inp["ml_f_bias"][l][None, :]
    sh["pp"] = pp
    sh["pr"] = pr
    sh["cst"], sh["ctab"] = _consts()
    return sh


def _prep_core(inp, b):
    h0 = np.zeros((T, D), np.float32)
    h0[112:128] = inp["meta_tokens"]
    h0[128:] = inp["x"][b]
    return {"xT": np.ascontiguousarray(h0.T).reshape(16, 128, T)}


def build_program(cfg=None):
    nc = bass.Bass("TRN2", target_bir_lowering=False)
    k = K(nc, cfg or {})
    return k.build()


def kernel(**inputs):
    inp = {k: np.asarray(v) for k, v in inputs.items()}
    sh = _prep_shared(inp)
    nc = build_program()
    in_maps = []
    for b in range(8):
        m = dict(sh)
        m.update(_prep_core(inp, b))
        in_maps.append(m)
    res = run_bass_kernel_spmd(nc, in_maps, core_ids=list(range(8)))
    out = np.empty((8, 2048, D), np.float32)
    for b in range(8):
        oT = res.results[b]["outT"].reshape(D, T)
        out[b] = oT[:, 128:].T
    return out
```
